# Optimizing a Trainium2 kernel written in Bass

```python
import jax, jax.numpy as jnp
from jax import lax
import numpy as np

D_MODEL = 1024
BATCH = 2
SEQ = 16384
DEPTH = 1

CHUNK = 64
SGU_BLOCK = 128
SGU_WIDTH = D_MODEL
SGU_GROUPS = 8
SGU_GROUP_DIM = SGU_WIDTH // SGU_GROUPS
RWKV_HEAD = 64
RWKV_WIDTH = D_MODEL
RWKV_HEADS = RWKV_WIDTH // RWKV_HEAD
DECAY_LORA = max(32, int(round(1.8 * D_MODEL ** 0.5 / 32)) * 32)
ICLR_LORA = max(32, int(round(1.8 * D_MODEL ** 0.5 / 32)) * 32)
GATE_LORA = max(32, int(round(0.6 * D_MODEL ** 0.8 / 32)) * 32)
D_FF = int(round(8 * D_MODEL / 3 / 128)) * 128
CONV_WIDTH = 3
LN_EPS = 1e-5
GN_EPS = 64e-5
ALPHA = (2.0 * DEPTH) ** 0.25
BETA = (8.0 * DEPTH) ** -0.25
SHIFT_W = 3 * RWKV_WIDTH + DECAY_LORA + ICLR_LORA + GATE_LORA
IN_COLS = 2 * SGU_WIDTH + SHIFT_W + 2 * D_MODEL
IN_SPLITS = [SGU_WIDTH, 2 * SGU_WIDTH, 2 * SGU_WIDTH + SHIFT_W]
RWKV_SPLITS = [RWKV_WIDTH, 2 * RWKV_WIDTH, 3 * RWKV_WIDTH, 3 * RWKV_WIDTH + DECAY_LORA,
               3 * RWKV_WIDTH + DECAY_LORA + ICLR_LORA]

kernel_name = 'hybrid_sgu_rwkv7_convffn_deepnorm'


def layer_norm(x, g, b, eps=LN_EPS):
    xf = x.astype(jnp.float32)
    mu = jnp.mean(xf, -1, keepdims=True)
    var = jnp.mean(jnp.square(xf - mu), -1, keepdims=True)
    return ((xf - mu) * lax.rsqrt(var + eps) * g + b).astype(x.dtype)


def token_shift(h):
    return jnp.pad(h, ((0, 0), (1, 0), (0, 0)))[:, :-1]


def sgu_mixer(u, v, ln_g, ln_b, w_s, b_s):
    B_, S_, _ = v.shape
    nblk = S_ // SGU_BLOCK
    v = layer_norm(v, ln_g, ln_b).reshape(B_, nblk, SGU_BLOCK, SGU_GROUPS, SGU_GROUP_DIM)
    chunk_id = jnp.arange(SGU_BLOCK) // CHUNK
    mask = chunk_id[:, None] >= chunk_id[None, :]
    w = jnp.where(mask[None], w_s, 0.0)
    z = jnp.einsum('gij,bnjgc->bnigc', w, v) + b_s.T[None, None, :, :, None]
    return u * z.reshape(B_, S_, SGU_WIDTH)


def rwkv7_mixer(r, k, val, wd, ad, gd, w0, w2, a0, a2, g2, k_k, k_a, r_k, lnx_g, lnx_b):
    B_, S_, C = r.shape
    H, N = RWKV_HEADS, RWKV_HEAD
    f32 = jnp.float32
    wlog = -jax.nn.softplus(-(w0.astype(f32) + jnp.tanh(wd.astype(f32)) @ w2.astype(f32))) - 0.5
    decay = jnp.exp(-jnp.exp(wlog))
    a = jax.nn.sigmoid(a0 + ad @ a2)
    g = jax.nn.sigmoid(gd) @ g2
    heads = lambda t: t.astype(f32).reshape(B_, S_, H, N)
    kk = heads(k * k_k)
    kk = kk / jnp.maximum(jnp.sqrt(jnp.sum(kk * kk, -1, keepdims=True)), 1e-12)
    a_h = heads(a)
    k_h = heads(k * (1 + (a - 1) * k_a))
    r_h, v_h, w_h = heads(r), heads(val), heads(decay)

    def step(state, inp):
        r_t, w_t, k_t, v_t, a_t, b_t = inp
        sa = jnp.einsum('bhij,bhj->bhi', state, a_t)
        state = (state * w_t[:, :, None, :] + sa[..., None] * b_t[:, :, None, :]
                 + v_t[..., None] * k_t[:, :, None, :])
        return state, jnp.einsum('bhij,bhj->bhi', state, r_t)

    tm = lambda t: jnp.swapaxes(t, 0, 1)
    state0 = jnp.zeros((B_, H, N, N), f32)
    _, y = lax.scan(step, state0, (tm(r_h), tm(w_h), tm(k_h), tm(v_h), tm(-kk), tm(kk * a_h)))
    y = jnp.swapaxes(y, 0, 1)
    mu = jnp.mean(y, -1, keepdims=True)
    var = jnp.mean(jnp.square(y - mu), -1, keepdims=True)
    y = ((y - mu) * lax.rsqrt(var + GN_EPS)).reshape(B_, S_, C) * lnx_g + lnx_b
    bonus = jnp.sum(r_h * k_h * r_k.astype(f32), -1, keepdims=True) * v_h
    y = (y + bonus.reshape(B_, S_, C)) * g
    return y.astype(r.dtype)


def token_mixer(h, w_in, b_gate, mu_shift, sgu_ln_g, sgu_ln_b, w_s, b_s, w0, w2, a0, a2, g2,
                k_k, k_a, r_k, lnx_g, lnx_b, w_o):
    proj = h @ w_in
    u, v, rk, gates = jnp.split(proj, IN_SPLITS, axis=-1)
    y_a = sgu_mixer(jax.nn.gelu(u, approximate=False), jax.nn.gelu(v, approximate=False),
                    sgu_ln_g, sgu_ln_b, w_s, b_s)
    rk = rk + (token_shift(rk) - rk) * mu_shift
    r, k, val, wd, ad, gd = jnp.split(rk, RWKV_SPLITS, axis=-1)
    y_b = rwkv7_mixer(r, k, val, wd, ad, gd, w0, w2, a0, a2, g2, k_k, k_a, r_k, lnx_g, lnx_b)
    gate_a, gate_b = jnp.split(jax.nn.sigmoid(gates + b_gate), 2, axis=-1)
    return (gate_a * y_a + gate_b * y_b) @ w_o


def conv_ffn(h, w_up, conv_w, conv_b, w_down):
    gate, val = jnp.split(h @ w_up, 2, axis=-1)
    S_ = h.shape[1]
    pad = jnp.pad(gate, ((0, 0), (CONV_WIDTH - 1, 0), (0, 0)))
    conv = sum((pad[:, i:i + S_] * conv_w[i] for i in range(CONV_WIDTH)), conv_b)
    return (jax.nn.gelu(conv, approximate=False) * val) @ w_down


def setup_inputs(seed: int = 0) -> dict:
    key = jax.random.key(seed)
    ks = jax.random.split(key, 32)
    L, D, C = DEPTH, D_MODEL, RWKV_WIDTH
    nrm = lambda k, shape, s: jax.random.normal(k, shape, jnp.float32) * s
    gain = lambda k, shape: 1.0 + nrm(k, shape, 0.02)
    return {
        'x': nrm(ks[0], (BATCH, SEQ, D), 1.0),
        'ln_in_g': gain(ks[1], (D,)),
        'ln_in_b': nrm(ks[2], (D,), 0.02),
        'w_in': nrm(ks[3], (L, D, IN_COLS), D ** -0.5),
        'b_gate': nrm(ks[4], (L, 2 * D), 0.1),
        'mu_shift': jax.random.uniform(ks[5], (L, SHIFT_W), jnp.float32),
        'sgu_ln_g': gain(ks[6], (L, SGU_WIDTH)),
        'sgu_ln_b': nrm(ks[7], (L, SGU_WIDTH), 0.02),
        'w_s': nrm(ks[8], (L, SGU_GROUPS, SGU_BLOCK, SGU_BLOCK), SGU_BLOCK ** -0.5),
        'b_s': gain(ks[9], (L, SGU_GROUPS, SGU_BLOCK)),
        'w0': jax.random.uniform(ks[10], (L, C), jnp.float32, -6.0, -1.0),
        'w2': nrm(ks[11], (L, DECAY_LORA, C), 0.5 * DECAY_LORA ** -0.5),
        'a0': nrm(ks[12], (L, C), 0.1),
        'a2': nrm(ks[13], (L, ICLR_LORA, C), 0.5 * ICLR_LORA ** -0.5),
        'g2': nrm(ks[14], (L, GATE_LORA, C), GATE_LORA ** -0.5),
        'k_k': 0.85 + nrm(ks[15], (L, C), 0.02),
        'k_a': gain(ks[16], (L, C)),
        'r_k': nrm(ks[17], (L, RWKV_HEADS, RWKV_HEAD), 0.1),
        'lnx_g': gain(ks[18], (L, C)),
        'lnx_b': nrm(ks[19], (L, C), 0.02),
        'w_o': nrm(ks[20], (L, D, D), BETA * D ** -0.5),
        'ln1_g': gain(ks[21], (L, D)),
        'ln1_b': nrm(ks[22], (L, D), 0.02),
        'w_up': nrm(ks[23], (L, D, 2 * D_FF), D ** -0.5),
        'conv_w': nrm(ks[24], (L, CONV_WIDTH, D_FF), CONV_WIDTH ** -0.5),
        'conv_b': nrm(ks[25], (L, D_FF), 0.02),
        'w_down': nrm(ks[26], (L, D_FF, D), BETA * D_FF ** -0.5),
        'ln2_g': gain(ks[27], (L, D)),
        'ln2_b': nrm(ks[28], (L, D), 0.02),
    }


def reference(x, ln_in_g, ln_in_b, w_in, b_gate, mu_shift, sgu_ln_g, sgu_ln_b, w_s, b_s,
              w0, w2, a0, a2, g2, k_k, k_a, r_k, lnx_g, lnx_b, w_o, ln1_g, ln1_b,
              w_up, conv_w, conv_b, w_down, ln2_g, ln2_b):
    x = layer_norm(x, ln_in_g, ln_in_b)
    for l in range(DEPTH):
        mix = token_mixer(x, w_in[l], b_gate[l], mu_shift[l], sgu_ln_g[l], sgu_ln_b[l], w_s[l], b_s[l],
                          w0[l], w2[l], a0[l], a2[l], g2[l], k_k[l], k_a[l], r_k[l], lnx_g[l], lnx_b[l],
                          w_o[l])
        x = layer_norm(ALPHA * x + mix, ln1_g[l], ln1_b[l])
        x = layer_norm(ALPHA * x + conv_ffn(x, w_up[l], conv_w[l], conv_b[l], w_down[l]), ln2_g[l], ln2_b[l])
    return x
```

```python
from contextlib import ExitStack
import os
import numpy as np
import ml_dtypes
import concourse.bass as bass
import concourse.mybir as mybir
from concourse.bass_utils import run_bass_kernel_spmd

F32 = mybir.dt.float32
BF16 = mybir.dt.bfloat16
AF = mybir.ActivationFunctionType
ALU = mybir.AluOpType
AX = mybir.AxisListType

NCORES = 8
D = 1024
SEQ = 16384
OWN = 4096
NCH = 34
ROWS = NCH * 128
NTOK = 33 * 128
RKW = 3360
INC = 7456
DFF = 2688
NFC = 21
LN_EPS = 1e-5
GN_EPS = 64e-5
ALPHA = 2.0 ** 0.25
SEMCH = 6000
BIS = int(os.environ.get('BIS', '0'))

PV = {}
_o = 0
for _n, _w in [("mu", 27), ("bg", 16), ("sg", 8), ("sb", 8), ("a0", 8), ("kk", 8), ("ka", 8),
               ("rk", 8), ("lxg", 8), ("lxb", 8), ("lig", 8), ("lib", 8), ("cw0", 21), ("cw1", 21),
               ("cw2", 21), ("cb", 21)]:
    PV[_n] = _o
    _o += _w
NPV = _o


class T:
    __slots__ = ("ap", "key")

    def __init__(self, ap, key):
        self.ap = ap
        self.key = key

    def __getitem__(self, idx):
        return T(self.ap[idx], self.key)

    def re(self, s, **kw):
        return T(self.ap.rearrange(s, **kw), self.key)


class Sched:
    ENGS = ("pe", "act", "dve", "pool", "sp")

    def __init__(self, nc, es):
        self.nc = nc
        self.es = es
        self.q = {e: [] for e in self.ENGS}
        self.cnt = {e: 0 for e in self.ENGS}
        self.bufs = {}
        self.slot_cnt = {}
        self.slot_sem = {}
        self.eng_sems = {e: [] for e in self.ENGS}
        self.pending_barrier = {e: None for e in self.ENGS}

    def _deps(self, reads, writes):
        deps = set()
        for k in reads:
            b = self.bufs.get(k)
            if b and b[0] is not None:
                deps.add(b[0])
        for k in writes:
            b = self.bufs.get(k)
            if b:
                if b[0] is not None:
                    deps.add(b[0])
                deps.update(b[1])
        return deps

    def _update(self, tok, reads, writes):
        for k in reads:
            if k in writes:
                continue
            b = self.bufs.setdefault(k, [None, []])
            b[1].append(tok)
            if len(b[1]) > 64:
                b[1] = self._prune(b[1])
        for k in writes:
            self.bufs[k] = [tok, []]

    @staticmethod
    def _prune(toks):
        best = {}
        for t in toks:
            kk = (t[0], t[1])
            if kk not in best or best[kk][2] < t[2]:
                best[kk] = t
        return list(best.values())

    def op(self, eng, fn, reads, writes):
        reads = [r for r in reads if r is not None]
        deps = self._deps(reads, writes)
        if self.pending_barrier[eng] is not None:
            deps |= self.pending_barrier[eng]
            self.pending_barrier[eng] = None
        self.cnt[eng] += 1
        tok = ("e", eng, self.cnt[eng])
        self._update(tok, reads, writes)
        self.q[eng].append((deps, fn, tok))
        return tok

    def dma(self, eng, out, in_, slot, serialize=True):
        reads = [in_.key] if in_.key is not None else []
        writes = [out.key] if out.key is not None else []
        deps = self._deps(reads, writes)
        if self.pending_barrier[eng] is not None:
            deps |= self.pending_barrier[eng]
            self.pending_barrier[eng] = None
        n = self.slot_cnt.get(slot, 0) + 1
        self.slot_cnt[slot] = n
        if n > 1 and serialize:
            deps.add(("d", slot, n - 1))
        tok = ("d", slot, n)
        self._update(tok, reads, writes)
        oa, ia = out.ap, in_.ap
        self.q[eng].append((deps, lambda e: e.dma_start(out=oa, in_=ia), tok))
        return tok

    def barrier(self):
        toks = set()
        for e in self.ENGS:
            if self.cnt[e] > 0:
                toks.add(("e", e, self.cnt[e]))
        for s, n in self.slot_cnt.items():
            toks.add(("d", s, n))
        for e in self.ENGS:
            self.pending_barrier[e] = set(toks) | (self.pending_barrier[e] or set())

    def alloc_sems(self):
        for e in self.ENGS:
            if e == "sp":
                continue
            nch = (self.cnt[e] + SEMCH - 1) // SEMCH
            for i in range(max(nch, 1)):
                self.eng_sems[e].append(self.es.enter_context(self.nc.semaphore(f"s_{e}{i}")))
        for s in self.slot_cnt:
            self.slot_sem[s] = self.es.enter_context(self.nc.semaphore("d_" + str(s).replace(" ", "")))

    def tok_sem(self, tok):
        if tok[0] == "e":
            idx = tok[2] - 1
            return self.eng_sems[tok[1]][idx // SEMCH], idx % SEMCH + 1
        return self.slot_sem[tok[1]], 16 * tok[2]

    def replay(self, eng, engine):
        waited = {}
        for deps, fn, tok in self.q[eng]:
            need = {}
            for d in deps:
                if d[0] == "e" and d[1] == eng and eng == "pe":
                    continue
                if d[0] == "e" and d[1] == eng and eng == "sp":
                    continue
                sem, val = self.tok_sem(d)
                sid = id(sem)
                if need.get(sid, (None, 0))[1] < val:
                    need[sid] = (sem, val)
            for sid, (sem, val) in need.items():
                if waited.get(sid, 0) >= val:
                    continue
                engine.wait_ge(sem, val)
                waited[sid] = val
            if fn is None:
                continue
            ins = fn(engine)
            sem, val = self.tok_sem(tok)
            if tok[0] == "e":
                ins.then_inc(sem, 1)
            else:
                ins.then_inc(sem, 16)


class KB:
    def __init__(self, nc, es, debug=False):
        self.nc = nc
        self.es = es
        self.S = Sched(nc, es)
        self.debug = debug
        self.ps_rr = 0
        self.uid = 0

    def sb(self, name, shape, dt, key=None):
        t = self.es.enter_context(self.nc.sbuf_tensor(name, list(shape), dt))
        return T(t[:] if False else t.ap() if hasattr(t, "ap") else t[:], key or name)

    def dram(self, name, shape, dt, kind="Internal"):
        return self.nc.dram_tensor(name, list(shape), dt, kind=kind).ap()

    def mm(self, out, lhsT, rhs, start=True, stop=True):
        o, l, r = out.ap, lhsT.ap, rhs.ap
        self.S.op("pe", lambda e: e.matmul(o, l, r, start=start, stop=stop),
                  [lhsT.key, rhs.key], [out.key])

    def act(self, out, in_, func, bias=None, scale=1.0, eng="act"):
        o, i = out.ap, in_.ap
        b = bias.ap if isinstance(bias, T) else bias
        s = scale.ap if isinstance(scale, T) else scale
        reads = [in_.key]
        if isinstance(bias, T):
            reads.append(bias.key)
        if isinstance(scale, T):
            reads.append(scale.key)
        if b is None:
            self.S.op("act", lambda e: e.activation(o, i, func, scale=s), reads, [out.key])
        else:
            self.S.op("act", lambda e: e.activation(o, i, func, bias=b, scale=s), reads, [out.key])

    def tt(self, eng, out, in0, in1, op):
        o, a, b = out.ap, in0.ap, in1.ap
        self.S.op(eng, lambda e: e.tensor_tensor(o, a, b, op), [in0.key, in1.key], [out.key])

    def ts(self, eng, out, in0, s1, s2, op0, op1=None):
        o, a = out.ap, in0.ap
        reads = [in0.key]
        v1 = s1.ap if isinstance(s1, T) else s1
        v2 = s2.ap if isinstance(s2, T) else s2
        if isinstance(s1, T):
            reads.append(s1.key)
        if isinstance(s2, T):
            reads.append(s2.key)
        if op1 is None:
            self.S.op(eng, lambda e: e.tensor_scalar(o, a, v1, None, op0), reads, [out.key])
        else:
            self.S.op(eng, lambda e: e.tensor_scalar(o, a, v1, v2, op0, op1), reads, [out.key])

    def stt(self, eng, out, in0, sc, in1, op0, op1):
        o, a, b = out.ap, in0.ap, in1.ap
        reads = [in0.key, in1.key]
        v = sc.ap if isinstance(sc, T) else sc
        if isinstance(sc, T):
            reads.append(sc.key)
        self.S.op(eng, lambda e: e.scalar_tensor_tensor(o, a, v, b, op0, op1), reads, [out.key])

    def copy(self, eng, out, in_):
        o, i = out.ap, in_.ap
        if eng == "act":
            self.S.op("act", lambda e: e.activation(o, i, AF.Copy), [in_.key], [out.key])
        else:
            self.S.op(eng, lambda e: e.tensor_copy(o, i), [in_.key], [out.key])

    def memset(self, eng, out, val):
        o = out.ap
        self.S.op(eng, lambda e: e.memset(o, val), [], [out.key])

    def dma(self, eng, out, in_, slot, serialize=True):
        self.S.dma(eng, out, in_, slot, serialize)


def _dram_T(ap, key=None):
    return T(ap, key)


def build_program(debug=False, ntiles=NCH // 2, do_scan=True, do_cc=True, nbt=8):
    nc = bass.Bass("TRN2", target_bir_lowering=False)
    es = ExitStack()
    with es:
        kb = KB(nc, es, debug)
        S = kb.S
        def din(name, shape, dt=F32):
            return nc.dram_tensor(name, list(shape), dt, kind="ExternalInput").ap()

        xs = din("xs", [ROWS, D])
        cmask_d = din("cmask", [128, 16])
        pvec_d = din("pvec", [128, NPV])
        w_in_d = din("w_in", [D, INC])
        if not debug:
            w_o_d = din("w_o", [D, D])
            w_up_d = din("w_up", [D, 2 * DFF])
            w_down_d = din("w_down", [DFF, D])
        w2_d = din("w2aug", [65, D])
        a2_d = din("a2", [64, D])
        g2_d = din("g2", [160, D])
        if not debug:
            wsT_d = din("w_sT", [8, 128, 128])
            bs_d = din("b_s", [8, 128])
            sgb_row_d = din("sgb_row", [8, 128])
            bc_d = din("bcast", [6, 128, D])
        cf_d = din("cf32", [128, 10 * 128])
        cb_d = din("cbf16", [128, 6 * 128], BF16)
        out_d = nc.dram_tensor("out", [OWN, D], F32, kind="ExternalOutput").ap()

        ydram = nc.dram_tensor("ydram", [D, NTOK], F32, kind="Internal").ap()
        rdram = nc.dram_tensor("rdram", [D, NTOK], BF16, kind="Internal").ap()
        bdram = nc.dram_tensor("bdram", [D, NTOK], BF16, kind="Internal").ap()
        gdram = nc.dram_tensor("gdram", [D, NTOK], BF16, kind="Internal").ap()
        wBd = nc.dram_tensor("wB_bf", [D, 4096], BF16, kind="Internal").ap()
        wod = nc.dram_tensor("wo_bf", [D, D], BF16, kind="Internal").ap()
        wud = nc.dram_tensor("wu_bf", [D, 2 * DFF], BF16, kind="Internal").ap()
        wdd = nc.dram_tensor("wd_bf", [DFF, D], BF16, kind="Internal").ap()
        cc_in = nc.dram_tensor("cc_in", [1024, 128], F32, kind="Internal").ap()
        cc_out = nc.dram_tensor("cc_out", [NCORES * 1024, 128], F32, kind="Internal").ap()
        if debug:
            dbg_y = nc.dram_tensor("dbg_y", [D, NTOK], F32, kind="ExternalOutput").ap()
            dbg_r = nc.dram_tensor("dbg_r", [D, NTOK], BF16, kind="ExternalOutput").ap()
            dbg_b = nc.dram_tensor("dbg_b", [D, NTOK], BF16, kind="ExternalOutput").ap()
            dbg_g = nc.dram_tensor("dbg_g", [D, NTOK], BF16, kind="ExternalOutput").ap()
            dbg_s = nc.dram_tensor("dbg_s", [1024, 128], F32, kind="ExternalOutput").ap()

        psb = []
        for b in range(8):
            t = es.enter_context(nc.psum_tensor(f"ps{b}", [128, 512], F32))
            psb.append(T(t[:], ("ps", b)))

        def ps():
            b = psb[kb.ps_rr % 8]
            kb.ps_rr += 1
            return b

        def sbt(name, shape, dt, key=None):
            t = es.enter_context(nc.sbuf_tensor(name, list(shape), dt))
            return T(t[:], key or name)

        cf = sbt("cf", [128, 10 * 128], F32)
        cbf = sbt("cbf", [128, 6 * 128], BF16)
        pvec = sbt("pvec_s", [128, NPV], F32)
        omm = sbt("omm", [128, 27], F32)
        cmask = sbt("cmask_s", [128, 16], F32)
        maskM = cf[:, 0:512]
        maskT4 = cf[:, 512:1024]
        tri_i = cf[:, 1024:1152]
        tri_x = cf[:, 1152:1280]
        ident = cbf[:, 0:128]
        ident4 = cbf[:, 0:512]
        blkb = cbf[:, 512:640]
        sgumask = cbf[:, 640:768]

        def pv(name, c=0, rows=128):
            return pvec[0:rows, PV[name] + c:PV[name] + c + 1]

        kb.dma("sp", cf, T(cf_d, None), "c0")
        kb.dma("sp", cbf, T(cb_d, None), "c1")
        kb.dma("sp", pvec, T(pvec_d, None), "c2")
        kb.dma("sp", cmask, T(cmask_d, None), "c3")
        kb.ts("dve", omm, pvec[:, PV["mu"]:PV["mu"] + 27], -1.0, 1.0, ALU.mult, ALU.add)
        epsc = sbt("epsc", [128, 4], F32)
        kb.memset("dve", epsc[:, 0:1], LN_EPS)
        kb.memset("dve", epsc[:, 1:2], 1e-24)
        kb.memset("dve", epsc[:, 2:3], GN_EPS)
        blkf = sbt("blkf", [128, 128], F32)
        kb.copy("dve", blkf, blkb)

        if not debug:
            WK = ("wbf",)
            for dc in range(8):
                rs_ = slice(dc * 128, (dc + 1) * 128)
                kb.dma("pool", T(wBd[rs_, 0:2048], WK), T(w_in_d[rs_, 0:2048], None), "wconv", serialize=False)
                kb.dma("pool", T(wBd[rs_, 2048:4096], WK), T(w_in_d[rs_, 5408:7456], None), "wconv", serialize=False)
                kb.dma("pool", T(wud[rs_, :], WK), T(w_up_d[rs_, :], None), "wconv", serialize=False)
                kb.dma("pool", T(wod[rs_, :], WK), T(w_o_d[rs_, :], None), "wconv", serialize=False)
            for fc in range(NFC):
                rs_ = slice(fc * 128, (fc + 1) * 128)
                kb.dma("pool", T(wdd[rs_, :], WK), T(w_down_d[rs_, :], None), "wconv", serialize=False)
        esA = ExitStack()
        es.enter_context(esA)

        def sba(name, shape, dt, key=None):
            t = esA.enter_context(nc.sbuf_tensor(name, list(shape), dt))
            return T(t[:], key or name)

        NA = 256
        wA = sba("wA", [128, 8, RKW], BF16)
        for dc in range(8):
            kb.dma("pool", T(wA.ap[:, dc, :], ("wA", dc)),
                   T(w_in_d[dc * 128:(dc + 1) * 128, 2048:2048 + RKW], None), ("wA", dc))
        w2b = sba("w2b", [65, D], BF16)
        kb.dma("pool", w2b, T(w2_d, None), "w2b")
        a2b = sba("a2b", [128, D], BF16)
        kb.dma("pool", a2b[64:128, :], T(a2_d, None), "a2b")
        g2b0 = sba("g2b0", [128, D], BF16)
        kb.dma("pool", g2b0, T(g2_d[0:128, :], None), "g2b0")
        g2b1 = sba("g2b1", [32, D], BF16)
        kb.dma("pool", g2b1, T(g2_d[128:160, :], None), "g2b1")

        xin = [sba(f"xin{i}", [128, D], F32) for i in range(1)]
        xnb = [sba(f"xnb{i}", [128, D], BF16) for i in range(1)]
        stats = sba("stats", [128, 2, 6], F32)
        mv = sba("mv", [128, 2], F32)
        rstd = sba("rstd", [128, 1], F32)
        omka = sba("omka", [128, 8], F32)
        kb.ts("dve", omka, pvec[:, PV["ka"]:PV["ka"] + 8], -1.0, 1.0, ALU.mult, ALU.add)
        hT = sba("hT", [128, 8, NA], BF16)
        prevc = sba("prevc", [128, 27], F32)
        kb.memset("pool", prevc, 0.0)
        tmpl = [sba(f"tmpl{i}", [128, NA], F32) for i in range(1)]
        wdad = sba("wdad", [128, NA], F32)
        gd0 = sba("gd0", [128, NA], F32)
        gd1 = sba("gd1", [32, NA], F32)
        thw = sba("thw", [65, NA], BF16)
        kb.memset("pool", thw, 1.0)
        sgd0 = sba("sgd0", [128, NA], BF16)
        sgd1 = sba("sgd1", [32, NA], BF16)
        adb = sba("adb", [128, NA], BF16)
        lwT = sba("lwT", [128, 2, D], F32)
        NT_ = 10
        ptmp = [[sba(f"pt{i}_{j}", [128, NA], F32) for j in range(NT_)] for i in range(1)]
        gout = [sba(f"gout{i}", [128, NA], BF16) for i in range(2)]
        bout = [sba(f"bout{i}", [128, NA], BF16) for i in range(2)]
        gLs = sba("gLs", [128, 8, 2], F32)
        ARt = sba("ARt", [128, 8, 2, 2, 128], BF16)
        Btt = sba("Btt", [128, 8, 2, 128], BF16)
        Ktt = sba("Ktt", [128, 8, 2, 128], BF16)
        Kht = sba("Kht", [128, 8, 2, 128], BF16)
        Bht = sba("Bht", [128, 8, 2, 128], BF16)
        Vtt = sba("Vtt", [128, 8, 2, 128], BF16)
        VaT = [sba(f"VaT{h}", [128, 8, 128], BF16) for h in range(2)]
        KhTz = [sba(f"KhTz{h}", [128, 8, 128], BF16) for h in range(2)]
        BhTz = [sba(f"BhTz{h}", [128, 8, 128], BF16) for h in range(2)]
        M3 = [sba(f"M3{h}", [128, 8, 384], BF16) for h in range(2)]
        Mp = [[sba(f"Mp{h}{q}", [128, 8, 128], BF16) for q in range(2)] for h in range(2)]
        MpT = [[sba(f"MpT{h}{q}", [128, 8, 128], BF16) for q in range(2)] for h in range(2)]
        Xb = [[sba(f"Xb{h}{q}", [128, 8, 128], BF16) for q in range(2)] for h in range(2)]
        WTb = [sba(f"WTb{h}", [128, 8, 128], BF16) for h in range(2)]
        UTb = [sba(f"UTb{h}", [128, 8, 128], BF16) for h in range(2)]
        Yst = [sba(f"Yst{h}", [64, 8, 128], F32) for h in range(2)]
        Rst = [sba(f"Rst{h}", [128, 8, 128], BF16) for h in range(2)]
        STb = [sba(f"STb{h}", [128, 4, 128], BF16) for h in range(2)]
        STf = [sba(f"STf{h}", [128, 4, 128], F32) for h in range(2)]
        for h in range(2):
            kb.memset("pool", VaT[h], 0.0)
            kb.memset("pool", KhTz[h], 0.0)
            kb.memset("pool", BhTz[h], 0.0)
            kb.memset("pool", STf[h], 0.0)
        for h in range(2):
            for p in range(4):
                kb.copy("pool", STf[h][0:64, p, 64:128], ident[0:64, 0:64])
                kb.copy("pool", STf[h][64:128, p, 64:128], ident[64:128, 64:128])
            kb.copy("pool", STb[h], STf[h])

        def phaseA_tile(tt):
            c0 = 2 * tt
            for sc in range(2):
                cc = c0 + sc
                xi = xin[0]
                xn = xnb[0]
                kb.dma("sp", xi, T(xs[cc * 128:(cc + 1) * 128, :], None), ("xin", 0))
                for hh in range(2):
                    o, i = stats.ap[:, hh, :], xi.ap[:, hh * 512:(hh + 1) * 512]
                    S.op("dve", lambda e, o=o, i=i: e.bn_stats(o, i), [xi.key], [stats.key])
                o, i = mv.ap, stats.ap
                S.op("dve", lambda e, o=o, i=i: e.bn_aggr(o, i), [stats.key], [mv.key])
                kb.act(rstd, mv[:, 1:2], AF.Ln, bias=epsc[:, 0:1])
                kb.act(rstd, rstd, AF.Exp, scale=-0.5)
                kb.ts("dve", xn, xi, mv[:, 0:1], rstd, ALU.subtract, ALU.mult)
                for g in range(2):
                    pb = ps()
                    for k in range(4):
                        dc = g * 4 + k
                        kb.mm(pb[:, k * 128:(k + 1) * 128], xn[:, dc * 128:(dc + 1) * 128], ident)
                    for k in range(4):
                        dc = g * 4 + k
                        kb.act(T(hT.ap[:, dc, sc * 128:(sc + 1) * 128], ("hT", sc)),
                               pb[:, k * 128:(k + 1) * 128], AF.Identity,
                               bias=pv("lib", dc), scale=pv("lig", dc))
            hTr = [T(hT.ap, ("hT", 0)), T(hT.ap, ("hT", 1))]

            def proj(fc, dst, rows=128):
                pb = ps()
                for dc in range(8):
                    o = pb[0:rows, 0:NA]
                    kb.S.op("pe", lambda e, o=o.ap, l=wA.ap[:, dc, fc * 128:fc * 128 + rows],
                            r=hT.ap[:, dc, :], st=(dc == 0), sp=(dc == 7):
                            e.matmul(o, l, r, start=st, stop=sp),
                            [("wA", dc), ("hT", 0), ("hT", 1)], [pb.key])
                tm = tmpl[0]
                kb.act(tm[0:rows, :], pb[0:rows, 0:NA], AF.Copy, scale=omm[0:rows, fc:fc + 1])
                kb.stt("dve", dst[0:rows, 1:NA], pb[0:rows, 0:NA - 1], pv("mu", fc, rows), tm[0:rows, 1:NA],
                       ALU.mult, ALU.add)
                kb.stt("dve", dst[0:rows, 0:1], prevc[0:rows, fc:fc + 1], pv("mu", fc, rows), tm[0:rows, 0:1],
                       ALU.mult, ALU.add)
                kb.copy("dve", prevc[0:rows, fc:fc + 1], pb[0:rows, NA - 1:NA])
                if tt == 0:
                    kb.ts("dve", prevc[0:rows, fc:fc + 1], prevc[0:rows, fc:fc + 1], cmask[0:rows, 0:1], None,
                          ALU.mult)

            proj(24, wdad)
            proj(25, gd0)
            proj(26, gd1, rows=32)
            kb.act(thw[0:64, :], wdad[0:64, :], AF.Tanh)
            kb.copy("dve", adb[64:128, :], wdad[64:128, :])
            kb.act(sgd0, gd0, AF.Sigmoid)
            kb.act(sgd1, gd1, AF.Sigmoid)
            for sc in range(2):
                for hh in range(2):
                    pb = ps()
                    kb.mm(pb, thw[0:65, sc * 128:(sc + 1) * 128], w2b[0:65, hh * 512:(hh + 1) * 512])
                    kb.act(T(lwT.ap[:, sc, hh * 512:(hh + 1) * 512], ("lwT", sc)), pb, AF.Sigmoid)
            for p in range(8):
                pt = ptmp[0]
                a_, r_, k_, ep, em, ex, kkt, t1, t2, t3 = pt
                cs = slice(p * 128, (p + 1) * 128)
                pb = ps()
                kb.mm(pb[:, 0:NA], a2b[64:128, cs], adb[64:128, :])
                kb.act(a_, pb[:, 0:NA], AF.Sigmoid, bias=pv("a0", p))
                pb = ps()
                kb.mm(pb[:, 0:NA], g2b0[:, cs], sgd0, start=True, stop=False)
                kb.mm(pb[:, 0:NA], g2b1[0:32, cs], sgd1[0:32, :], start=False, stop=True)
                go = gout[p % 2]
                kb.copy("act", go, pb[:, 0:NA])
                for sc in range(2):
                    cc = c0 + sc
                    if cc >= 1:
                        kb.dma("sp", T(gdram[cs, (cc - 1) * 128:cc * 128], ("gdram", p, cc)),
                               go[:, sc * 128:(sc + 1) * 128], ("gout", p % 2, sc))
                pb = ps()
                for sc in range(2):
                    lw_ = T(lwT.ap[:, sc, cs], ("lwT", sc))
                    kb.mm(pb[:, sc * 128:(sc + 1) * 128], lw_, tri_i)
                    kb.mm(pb[:, 256 + sc * 128:256 + (sc + 1) * 128], lw_, tri_x)
                kb.act(ep, pb[:, 0:NA], AF.Exp)
                kb.act(em, pb[:, 0:NA], AF.Exp, scale=-1.0)
                kb.act(ex, pb[:, 256:256 + NA], AF.Exp)
                for sc in range(2):
                    kb.copy("pool", T(gLs.ap[:, p, sc:sc + 1], ("gLs", p)), ep[:, sc * 128 + 127:sc * 128 + 128])
                vb = T(Vtt.ap[:, p].rearrange("q c t -> q (c t)"), ("vt", p))
                proj(p, r_)
                proj(8 + p, k_)
                proj(16 + p, vb)
                if tt == 0:
                    kb.ts("dve", vb[:, 128:256], vb[:, 128:256], cmask[:, 0:1], None, ALU.mult)
                kb.act(kkt, k_, AF.Copy, scale=pv("kk", p))
                kb.act(t1, k_, AF.Square, scale=pv("kk", p))
                pb = ps()
                kb.mm(pb[:, 0:NA], blkf, t1)
                kb.act(t2, pb[:, 0:NA], AF.Ln, bias=epsc[:, 1:2])
                kb.act(t2, t2, AF.Exp, scale=-0.5)
                kb.tt("dve", kkt, kkt, t2, ALU.mult)
                kb.act(t1, a_, AF.Identity, bias=omka[:, p:p + 1], scale=pv("ka", p))
                kb.tt("pool", t1, t1, k_, ALU.mult)
                kb.stt("dve", t3, r_, pv("rk", p), t1, ALU.mult, ALU.mult)
                pb = ps()
                kb.mm(pb[:, 0:NA], blkf, t3)
                bo = bout[p % 2]
                kb.tt("dve", bo, pb[:, 0:NA], vb, ALU.mult)
                for sc in range(2):
                    cc = c0 + sc
                    if cc >= 1:
                        kb.dma("sp", T(bdram[cs, (cc - 1) * 128:cc * 128], ("bdram", p, cc)),
                               bo[:, sc * 128:(sc + 1) * 128], ("bout", p % 2, sc))
                key = ("scanop", p)
                v4 = lambda t_: t_.ap.rearrange("q (c t) -> q c t", c=2)
                kb.tt("pool", t2, kkt, a_, ALU.mult)
                kb.stt("dve", T(ARt.ap[:, p, :, 0, :], key), T(v4(kkt), kkt.key), -1.0, T(v4(ex), ex.key),
                       ALU.mult, ALU.mult)
                kb.tt("pool", T(ARt.ap[:, p, :, 1, :], key), T(v4(r_), r_.key), T(v4(ep), ep.key), ALU.mult)
                kb.tt("dve", T(Btt.ap[:, p], key), T(v4(t2), t2.key), T(v4(em), em.key), ALU.mult)
                kb.tt("pool", T(Ktt.ap[:, p], key), T(v4(t1), t1.key), T(v4(em), em.key), ALU.mult)
                for sc in range(2):
                    gl = T(gLs.ap[:, p, sc:sc + 1], ("gLs", p))
                    kb.act(T(Kht.ap[:, p, sc], key), T(Ktt.ap[:, p, sc], key), AF.Copy, scale=gl)
                    kb.act(T(Bht.ap[:, p, sc], key), T(Btt.ap[:, p, sc], key), AF.Copy, scale=gl)

        def scan_prep(cc, sc, hf, res):
            P0 = hf * 4
            keys = [("scanop", P0 + q) for q in range(4)]
            for src, dst, mode in ((Vtt, VaT[hf], 0), (Kht, KhTz[hf], 1), (Bht, BhTz[hf], 1)) if not (BIS & 8) else ():
                pb = ps()
                for q in range(4):
                    kq = ("vt", P0 + q) if mode == 0 else keys[q]
                    kb.mm(pb[:, q * 128:(q + 1) * 128], T(src.ap[:, P0 + q, sc, :], kq), ident)
                pv_ = pb.ap.rearrange("s (q e j) -> s q e j", q=4, e=2)
                dv = dst.ap.rearrange("s (q e) c -> s q e c", e=2)
                if mode == 0:
                    kb.copy("dve", T(dv[:, :, 0, 0:64], dst.key), T(pv_[:, :, 0, :], pb.key))
                    kb.copy("dve", T(dv[:, :, 1, 0:64], dst.key), T(pv_[:, :, 1, :], pb.key))
                else:
                    kb.copy("dve", T(dv[:, :, 0, 0:64], dst.key), T(pv_[:, :, 0, :], pb.key))
                    kb.copy("dve", T(dv[:, :, 1, 64:128], dst.key), T(pv_[:, :, 1, :], pb.key))
            yield
            if BIS & 16:
                res[hf] = 0
                return
            for hl in range(8 if not (BIS & 32) else 0):
                p, e = P0 + hl // 2, hl % 2
                rs = slice(e * 64, (e + 1) * 64)
                k_ = ("scanop", p)
                pb = ps()
                AR = T(ARt.ap[rs, p, sc].rearrange("j a t -> j (a t)"), k_)
                kb.mm(pb[:, 0:256], T(Btt.ap[rs, p, sc, :], k_), AR)
                kb.mm(pb[:, 256:512], T(Ktt.ap[rs, p, sc, :], k_), AR)
                kb.tt("dve", Mp[hf][0][:, hl, :], pb[:, 0:128], maskM[:, 0:128], ALU.mult)
                kb.tt("dve", M3[hf][:, hl, :], pb[:, 128:512], maskM[:, 128:512], ALU.mult)
                if hl % 2 == 1:
                    yield
            for e in range(2 if not (BIS & 64) else 0):
                pb = ps()
                rs = slice(e * 64, (e + 1) * 64)
                for q in range(4):
                    p = P0 + q
                    k_ = ("scanop", p)
                    kb.mm(pb[:, q * 128:(q + 1) * 128], T(ARt.ap[rs, p, sc, 0, :], k_), T(Btt.ap[rs, p, sc, :], k_))
                kb.tt("dve", T(MpT[hf][0].ap.rearrange("s (q e) t -> s q e t", e=2)[:, :, e, :], MpT[hf][0].key),
                      pb.re("s (q t) -> s q t", q=4), T(maskT4.ap.rearrange("s (q t) -> s q t", q=4), maskT4.key), ALU.mult)
                yield
            cur = 0
            for g in range(2 if not (BIS & 128) else 0):
                kb.tt("pool", Xb[hf][0][:, g * 4:(g + 1) * 4, :].re("s h t -> s (h t)"),
                      Mp[hf][0][:, g * 4:(g + 1) * 4, :].re("s h t -> s (h t)"), ident4, ALU.add)
            xcur = 0
            for rd in range(1, 8 if not (BIS & 2) else 1):
                nxt = 1 - cur
                for g in range(2):
                    hs = slice(g * 4, (g + 1) * 4)
                    if rd <= 6:
                        if rd <= 5:
                            pb1 = ps()
                            for q in range(4):
                                hl = g * 4 + q
                                kb.mm(pb1[:, q * 128:(q + 1) * 128], MpT[hf][cur][:, hl, :], Mp[hf][cur][:, hl, :])
                        pb2 = ps()
                        for q in range(4):
                            hl = g * 4 + q
                            kb.mm(pb2[:, q * 128:(q + 1) * 128], Mp[hf][cur][:, hl, :], MpT[hf][cur][:, hl, :])
                    if rd >= 2:
                        pb3 = ps()
                        for q in range(4):
                            hl = g * 4 + q
                            kb.mm(pb3[:, q * 128:(q + 1) * 128], MpT[hf][cur][:, hl, :], Xb[hf][xcur][:, hl, :])
                    if rd <= 5:
                        kb.copy("act", Mp[hf][nxt][:, hs, :].re("s h t -> s (h t)"), pb1)
                    if rd <= 6:
                        kb.copy("act" if rd % 2 else "dve", MpT[hf][nxt][:, hs, :].re("s h t -> s (h t)"), pb2)
                    if rd >= 2:
                        kb.tt("dve", Xb[hf][1 - xcur][:, hs, :].re("s h t -> s (h t)"), pb3,
                              Xb[hf][xcur][:, hs, :].re("s h t -> s (h t)"), ALU.add)
                    yield
                if rd >= 2:
                    xcur = 1 - xcur
                cur = nxt
            res[hf] = xcur

        def scan_serial(cc, sc, hf, xcur):
            P0 = hf * 4
            Xi = Xb[hf][xcur]
            stb, stf = STb[hf], STf[hf]
            for e in range(2):
                pb = ps()
                rs = slice(e * 64, (e + 1) * 64)
                for q in range(4):
                    hl = 2 * q + e
                    p = P0 + q
                    k_ = ("scanop", p)
                    o = pb[:, q * 128:(q + 1) * 128]
                    kb.mm(o, T(ARt.ap[rs, p, sc, 0, :], k_), stb[rs, q, :], start=True, stop=False)
                    kb.mm(o, M3[hf][:, hl, 128:256], VaT[hf][:, hl, :], start=False, stop=True)
                kb.copy("act" if e == 0 else "dve",
                        T(WTb[hf].ap.rearrange("s (q e) t -> s q e t", e=2)[:, :, e, :], WTb[hf].key),
                        pb.re("s (q t) -> s q t", q=4))
                yield
            for g in range(2):
                pb = ps()
                for q in range(4):
                    hl = g * 4 + q
                    kb.mm(pb[:, q * 128:(q + 1) * 128], Xi[:, hl, :], WTb[hf][:, hl, :])
                kb.copy("act" if g == 0 else "dve", UTb[hf][:, g * 4:(g + 1) * 4, :].re("s h t -> s (h t)"), pb)
                yield
            for e in range(2):
                pb = ps()
                rs = slice(e * 64, (e + 1) * 64)
                for q in range(4):
                    hl = 2 * q + e
                    p = P0 + q
                    k_ = ("scanop", p)
                    o = pb[:, q * 128:(q + 1) * 128]
                    kb.mm(o, stb[rs, q, :], T(ARt.ap[rs, p, sc, 1, :], k_), start=True, stop=False)
                    kb.mm(o, UTb[hf][:, hl, :], M3[hf][:, hl, 0:128], start=False, stop=False)
                    kb.mm(o, VaT[hf][:, hl, :], M3[hf][:, hl, 256:384], start=False, stop=True)
                kb.copy("act", T(Yst[hf].ap.rearrange("s (q e) t -> s q e t", e=2)[0:64, :, e, :], Yst[hf].key),
                        pb[0:64, :].re("s (q t) -> s q t", q=4))
                kb.copy("dve", T(Rst[hf].ap.rearrange("s (q e) t -> s q e t", e=2)[64:128, :, e, :], Rst[hf].key),
                        pb[64:128, :].re("s (q t) -> s q t", q=4))
            col = (cc - 1) * 128
            yv = ydram[hf * 512:(hf + 1) * 512, col:col + 128].rearrange("(h i) t -> i h t", i=64)
            rv = rdram[hf * 512:(hf + 1) * 512, col:col + 128].rearrange("(h i) t -> i h t", i=64)
            if not (BIS & 4):
                kb.dma("sp", T(yv, ("ydram", hf, cc)), Yst[hf][0:64], ("yst", hf, 0))
                kb.dma("sp", T(rv, ("rdram", hf, cc)), Rst[hf][64:128], ("yst", hf, 1))
            yield
            pb = ps()
            for q in range(4):
                o = pb[:, q * 128:(q + 1) * 128]
                for e in range(2):
                    hl = q * 2 + e
                    kb.mm(o, BhTz[hf][:, hl, :], UTb[hf][:, hl, :], start=(e == 0), stop=False)
                    kb.mm(o, KhTz[hf][:, hl, :], VaT[hf][:, hl, :], start=False, stop=(e == 1))
            for q in range(4):
                gl = T(gLs.ap[:, P0 + q, sc:sc + 1], ("gLs", P0 + q))
                kb.stt("dve", stf[:, q, :], stf[:, q, :], gl, pb[:, q * 128:(q + 1) * 128], ALU.mult, ALU.add)
            kb.copy("act", stb, stf)
            yield

        def run_il(gens):
            gens = list(gens)
            while gens:
                for g_ in list(gens):
                    try:
                        next(g_)
                    except StopIteration:
                        gens.remove(g_)

        for tt in range(ntiles):
            phaseA_tile(tt)
            for sc in range(2):
                cc = 2 * tt + sc
                if cc == 0 or not do_scan:
                    continue
                res = {}
                run_il([scan_prep(cc, sc, hf, res) for hf in range(2)])
                if not (BIS & 1):
                    run_il([scan_serial(cc, sc, hf, res[hf]) for hf in range(2)])
                if cc == 32:
                    for hf in range(2):
                        kb.dma("sp", T(cc_in[hf * 512:(hf + 1) * 512, :].rearrange("(q p) c -> p q c", p=128), ("cc_in", hf)),
                               STf[hf], ("ccin", hf))
        S.barrier()
        cc_tok_holder = []

        def emit_cc(e):
            return e.collective_compute("AllGather", ALU.bypass, replica_groups=[list(range(NCORES))],
                                        ins=[cc_in[:, :]], outs=[cc_out[:, :]])
        if do_cc:
            S.op("pool", emit_cc, [("cc_in", 0), ("cc_in", 1)], [("cc_out",)])
        else:
            for r_ in range(64):
                kb.dma("sp", T(cc_out[r_ * 128:(r_ + 1) * 128, :], ("cc_out",)), cf[:, 0:128], "ccfake")
        S.barrier()
        if debug:
            ncol = 128 * (2 * ntiles - 1)
            for ci in range(0, ncol if not os.environ.get("NODBG") else 0, 128):
                cs_ = slice(ci, ci + 128)
                if do_scan and not (BIS & 5):
                    kb.dma("sp", T(dbg_y[:, cs_], ("dbg", 0)), T(ydram[:, cs_], None), "dbg0")
                    kb.dma("sp", T(dbg_r[:, cs_], ("dbg", 1)), T(rdram[:, cs_], None), "dbg1")
                kb.dma("sp", T(dbg_b[:, cs_], ("dbg", 2)), T(bdram[:, cs_], None), "dbg2")
                kb.dma("sp", T(dbg_g[:, cs_], ("dbg", 3)), T(gdram[:, cs_], None), "dbg3")
            for hf in range(2):
                kb.dma("sp", T(dbg_s[hf * 512:(hf + 1) * 512, :].rearrange("(q p) c -> p q c", p=128), ("dbg", 4)),
                       STf[hf], ("dbg4", hf))
            S.barrier()

        if not debug:
            esA.close()
            esB = ExitStack()
            es.enter_context(esB)

            def sbb(name, shape, dt, key=None):
                t = esB.enter_context(nc.sbuf_tensor(name, list(shape), dt))
                return T(t[:], key or name)

            NB = 512
            bcs = [sbb(f"bc{i}", [128, D], F32) for i in range(6)]
            for i in range(6):
                kb.dma("sp", bcs[i], T(bc_d[i], None), ("bc", i))
            wsT = sbb("wsT", [128, 8, 128], BF16)
            gv = sbb("gv", [128, D], F32)
            wsTf = gv.re("p (g i) -> p g i", g=8)
            kb.dma("sp", wsTf, T(wsT_d.rearrange("g j i -> j g i"), None), "wsTf")
            for g in range(8):
                kb.tt("dve", wsT[:, g, :], wsTf[:, g, :], sgumask, ALU.mult)
            onesb = sbb("onesb", [128, 2], BF16)
            kb.memset("dve", onesb, 1.0)
            l2 = sbb("l2", [2, 8, 128], F32)
            r2 = sbb("r2", [2, 8, 128], F32)
            kb.memset("dve", l2, 1.0)
            kb.dma("sp", l2[0:1], T(sgb_row_d.rearrange("(o g) c -> o g c", o=1), None), "l2")
            kb.dma("sp", r2[1:2], T(bs_d.rearrange("(o g) c -> o g c", o=1), None), "r2")
            Bias = sbb("Bias", [128, 8, 128], F32)
            for g in range(8):
                pb = ps()
                kb.mm(pb[0:1, 0:128], onesb[:, 0:1], wsT[:, g, :])
                kb.copy("dve", r2[0:1, g, :], pb[0:1, 0:128])
            for g in range(8):
                pb = ps()
                kb.mm(pb[:, 0:128], l2[0:2, g, :], r2[0:2, g, :])
                kb.copy("dve", Bias[:, g, :], pb[:, 0:128])
            Gt = [sbb(f"Gt{i}", [128, 128], F32) for i in range(2)]
            PTbd = sbb("PTbd", [128, 128], F32)
            kb.memset("dve", PTbd, 0.0)
            identf2 = sbb("identf2", [128, 128], F32)
            kb.copy("dve", identf2, ident)
            Pbd = sbb("Pbd", [128, 128], F32)
            Scur = sbb("Scur", [128, 64], F32)
            SinT = sbb("SinT", [128, 8, 64], F32)
            kb.memset("dve", SinT, 0.0)
            Sinbd = sbb("Sinbd", [128, 8, 128], BF16)
            kb.memset("dve", Sinbd, 0.0)
            gi = 0
            for bb in range(2):
                for p in range(8):
                    for k in range(1, 4):
                        c = 4 * bb + k - 1
                        gt = Gt[gi % 2]
                        gi += 1
                        r0 = c * 1024 + p * 128
                        kb.dma("sp", gt, T(cc_out[r0:r0 + 128, :], ("cc_out",)), ("gt", gi % 2))
                        if k == 1:
                            kb.copy("dve", Scur, gt[:, 0:64])
                        else:
                            kb.copy("dve", PTbd[0:64, 0:64], gt[0:64, 64:128])
                            kb.copy("dve", PTbd[64:128, 64:128], gt[64:128, 64:128])
                            pb = ps()
                            kb.mm(pb[:, 0:128], PTbd, identf2)
                            kb.copy("dve", Pbd, pb[:, 0:128])
                            pb = ps()
                            kb.mm(pb[:, 0:64], Pbd, Scur)
                            kb.tt("dve", Scur, pb[:, 0:64], gt[:, 0:64], ALU.add)
                        idx = bb * 3 + k - 1
                        kb.stt("dve", SinT[:, p, :], Scur, cmask[:, 1 + idx:2 + idx], SinT[:, p, :], ALU.mult, ALU.add)
            for p in range(8):
                kb.copy("dve", Sinbd[0:64, p, 0:64], SinT[0:64, p, :])
                kb.copy("dve", Sinbd[64:128, p, 64:128], SinT[64:128, p, :])

            wBbs = [sbb(f"wBb{i}", [128, 8, D], BF16) for i in range(2)]
            xt = sbb("xt", [128, 4, D], F32)
            xnb2 = sbb("xnb2", [128, D], BF16)
            xT = sbb("xT", [128, 8, NB], BF16)
            ufm = sbb("ufm", [128, 8, NB], BF16)
            gaf = sbb("gaf", [128, 8, NB], BF16)
            gbf = sbb("gbf", [128, 8, NB], BF16)
            vn = sbb("vn", [128, D], BF16)
            actb = sbb("actb", [128, NFC, NB], BF16)
            Gb = [sbb(f"Gb{i}", [128, NB + 2], F32) for i in range(2)]
            ct = [sbb(f"ct{i}", [128, NB], F32) for i in range(1)]
            Ghalo = sbb("Ghalo", [128, NFC, 2], F32)
            kb.memset("dve", Ghalo, 0.0)
            NWU, NWD = 3, 4
            wub = [[sbb(f"wub{i}{j}", [128, 8, 128], BF16) for j in range(2)] for i in range(NWU)]
            wdb = [sbb(f"wdb{i}", [128, D], BF16) for i in range(NWD)]
            tiles_B = [(1, 1, True)] + [(2 + 4 * tb, 4, False) for tb in range(nbt)]
            PIECE_COL = [0, 1024, 2048, 3072, -1]
            st_ = {"wB": 0, "wu": 0, "wd": 0}
            n_wB = 5 * len(tiles_B)
            n_wu = NFC * len(tiles_B)
            n_wd = NFC * (len(tiles_B) - 1)
            wudv = wud.rearrange("(dc q) c -> q dc c", q=128)

            def pre_wB(k):
                while st_["wB"] <= min(k, n_wB - 1):
                    i = st_["wB"]
                    buf = wBbs[i % 2]
                    c0_ = PIECE_COL[i % 5]
                    for dc in range(8):
                        src_ = wod[dc * 128:(dc + 1) * 128, :] if c0_ < 0 else wBd[dc * 128:(dc + 1) * 128, c0_:c0_ + 1024]
                        kb.dma("sp", T(buf.ap[:, dc, :], (buf.key, dc)), T(src_, WK), ("wBb", i % 2, dc))
                    st_["wB"] += 1

            def pre_wu(k):
                while st_["wu"] <= min(k, n_wu - 1):
                    i = st_["wu"]
                    fc = i % NFC
                    wg, wv = wub[i % NWU]
                    kb.dma("sp", wg, T(wudv[:, :, fc * 128:(fc + 1) * 128], WK), ("wub", i % NWU, 0))
                    kb.dma("sp", wv, T(wudv[:, :, DFF + fc * 128:DFF + (fc + 1) * 128], WK), ("wub", i % NWU, 1))
                    st_["wu"] += 1

            def pre_wd(k):
                while st_["wd"] <= min(k, n_wd - 1):
                    i = st_["wd"]
                    fc = i % NFC
                    kb.dma("sp", wdb[i % NWD], T(wdd[fc * 128:(fc + 1) * 128, :], WK), ("wdb", i % NWD))
                    st_["wd"] += 1
            cnt_ = {"wB": 0, "wu": 0, "wd": 0}
            yt = [sbb(f"yt{i}", [128, NB], F32) for i in range(3)]
            rhb = sbb("rhb", [128, NB], BF16)
            bob = sbb("bob", [128, NB], BF16)
            gob = sbb("gob", [128, NB], BF16)
            st2 = sbb("st2", [128, 2, 6], F32)
            mv2 = sbb("mv2", [128, 2], F32)
            rs2 = sbb("rs2", [128, 1], F32)
            nm2 = sbb("nm2", [128, 1], F32)
            ztmp = sbb("ztmp", [128, 128], F32)

            def ln_rows(src, dst_f32, gB, bB, dst_bf=None):
                for hh in range(2):
                    o, i = st2.ap[:, hh, :], src.ap[:, hh * 512:(hh + 1) * 512]
                    S.op("dve", lambda e, o=o, i=i: e.bn_stats(o, i), [src.key], [st2.key])
                o, i = mv2.ap, st2.ap
                S.op("dve", lambda e, o=o, i=i: e.bn_aggr(o, i), [st2.key], [mv2.key])
                kb.act(rs2, mv2[:, 1:2], AF.Ln, bias=epsc[:, 0:1])
                kb.act(rs2, rs2, AF.Exp, scale=-0.5)
                kb.ts("dve", dst_f32, src, mv2[:, 0:1], rs2, ALU.subtract, ALU.mult)
                if dst_bf is not None:
                    kb.copy("act", dst_bf, dst_f32)
                kb.tt("dve", dst_f32, dst_f32, gB, ALU.mult)
                kb.tt("pool", dst_f32, dst_f32, bB, ALU.add)

            def phaseB_tile(c0, nsub, is_halo):
                N = nsub * 128
                for sb_ in range(nsub):
                    cc = c0 + sb_
                    xs_t = T(xt.ap[:, sb_, :], ("xt", sb_))
                    kb.dma("sp", xs_t, T(xs[cc * 128:(cc + 1) * 128, :], None), ("xt", sb_))
                    ln_rows(xs_t, xs_t, bcs[0], bcs[1], dst_bf=xnb2)
                    for g in range(2):
                        pb = ps()
                        for k in range(4):
                            dc = g * 4 + k
                            kb.mm(pb[:, k * 128:(k + 1) * 128], xnb2[:, dc * 128:(dc + 1) * 128], ident)
                        for k in range(4):
                            dc = g * 4 + k
                            kb.act(T(xT.ap[:, dc, sb_ * 128:(sb_ + 1) * 128], ("xT", sb_)),
                                   pb[:, k * 128:(k + 1) * 128], AF.Identity, bias=pv("lib", dc), scale=pv("lig", dc))
                xTk = [("xT", i) for i in range(nsub)]

                cur = {}

                def load_wB(col0):
                    k = cnt_["wB"]
                    cnt_["wB"] += 1
                    pre_wB(k + 1)
                    cur["wBb"] = wBbs[k % 2]

                def proj_fm(dst, func, bias_name):
                    wBb = cur["wBb"]
                    for g in range(8):
                        pb = ps()
                        for dc in range(8):
                            o, l, r = pb.ap[:, 0:N], wBb.ap[:, dc, g * 128:(g + 1) * 128], xT.ap[:, dc, 0:N]
                            S.op("pe", lambda e, o=o, l=l, r=r, st=(dc == 0), sp=(dc == 7):
                                 e.matmul(o, l, r, start=st, stop=sp), [(wBb.key, dc)] + xTk, [pb.key])
                        if bias_name is None:
                            kb.act(T(dst.ap[:, g, 0:N], (dst.key, g)), pb[:, 0:N], func)
                        else:
                            kb.act(T(dst.ap[:, g, 0:N], (dst.key, g)), pb[:, 0:N], func, bias=bias_name(g))
                load_wB(0)
                proj_fm(ufm, AF.Gelu, None)
                load_wB(1024)
                for sb_ in range(nsub):
                    for hh in range(2):
                        pb = ps()
                        wBb = cur["wBb"]
                        for dc in range(8):
                            o, l, r = pb.ap, xT.ap[:, dc, sb_ * 128:(sb_ + 1) * 128], wBb.ap[:, dc, hh * 512:(hh + 1) * 512]
                            S.op("pe", lambda e, o=o, l=l, r=r, st=(dc == 0), sp=(dc == 7):
                                 e.matmul(o, l, r, start=st, stop=sp), [(wBb.key, dc), ("xT", sb_)], [pb.key])
                        kb.act(gv[:, hh * 512:(hh + 1) * 512], pb, AF.Gelu)
                    for hh in range(2):
                        o, i = st2.ap[:, hh, :], gv.ap[:, hh * 512:(hh + 1) * 512]
                        S.op("dve", lambda e, o=o, i=i: e.bn_stats(o, i), [gv.key], [st2.key])
                    o, i = mv2.ap, st2.ap
                    S.op("dve", lambda e, o=o, i=i: e.bn_aggr(o, i), [st2.key], [mv2.key])
                    kb.act(rs2, mv2[:, 1:2], AF.Ln, bias=epsc[:, 0:1])
                    kb.act(rs2, rs2, AF.Exp, scale=-0.5)
                    kb.ts("dve", vn, gv, mv2[:, 0:1], rs2, ALU.subtract, ALU.mult)
                    for g in range(8):
                        pb = ps()
                        kb.mm(pb[:, 0:128], vn[:, g * 128:(g + 1) * 128], wsT[:, g, :])
                        kb.stt("dve", ztmp, pb[:, 0:128], pv("sg", g), Bias[:, g, :], ALU.mult, ALU.add)
                        uu = T(ufm.ap[:, g, sb_ * 128:(sb_ + 1) * 128], ("ufm", g))
                        kb.tt("pool", uu, ztmp, uu, ALU.mult)
                load_wB(2048)
                proj_fm(gaf, AF.Sigmoid, lambda g: pv("bg", g))
                load_wB(3072)
                proj_fm(gbf, AF.Sigmoid, lambda g: pv("bg", 8 + g))
                for g in range(8):
                    a_ = T(gaf.ap[:, g, 0:N], ("gaf", g))
                    kb.tt("pool", a_, a_, T(ufm.ap[:, g, 0:N], ("ufm", g)), ALU.mult)
                col = (c0 - 1) * 128
                for p in range(8):
                    cs = slice(p * 128, (p + 1) * 128)
                    y_, yc, sq = [t_[:, 0:N] for t_ in yt]
                    tq = sq
                    kb.dma("sp", y_, T(ydram[cs, col:col + N], None), "ld_y")
                    kb.dma("sp", rhb[:, 0:N], T(rdram[cs, col:col + N], None), "ld_r")
                    kb.dma("sp", bob[:, 0:N], T(bdram[cs, col:col + N], None), "ld_b")
                    kb.dma("sp", gob[:, 0:N], T(gdram[cs, col:col + N], None), "ld_g")
                    pb = ps()
                    kb.mm(pb[:, 0:N], Sinbd[:, p, :], rhb[:, 0:N])
                    kb.tt("dve", y_, pb[:, 0:N], y_, ALU.add)
                    pb = ps()
                    kb.mm(pb[:, 0:N], blkf, y_)
                    kb.stt("dve", yc, pb[:, 0:N], -1.0 / 64, y_, ALU.mult, ALU.add)
                    kb.act(sq, yc, AF.Square)
                    pb = ps()
                    kb.mm(pb[:, 0:N], blkf, sq)
                    kb.act(tq, pb[:, 0:N], AF.Ln, bias=epsc[:, 2:3], scale=1.0 / 64)
                    kb.act(tq, tq, AF.Exp, scale=-0.5)
                    kb.tt("dve", yc, yc, tq, ALU.mult)
                    kb.act(yc, yc, AF.Identity, bias=pv("lxb", p), scale=pv("lxg", p))
                    kb.tt("pool", yc, yc, bob[:, 0:N], ALU.add)
                    kb.tt("dve", yc, yc, gob[:, 0:N], ALU.mult)
                    b_ = T(gbf.ap[:, p, 0:N], ("gbf", p))
                    kb.tt("pool", yc, yc, b_, ALU.mult)
                    kb.tt("dve", b_, yc, T(gaf.ap[:, p, 0:N], ("gaf", p)), ALU.add)
                load_wB(-1)
                wo = cur["wBb"]
                for sb_ in range(nsub):
                    xs_t = T(xt.ap[:, sb_, :], ("xt", sb_))
                    for hh in range(2):
                        pb = ps()
                        for c in range(8):
                            o, l, r = pb.ap, gbf.ap[:, c, sb_ * 128:(sb_ + 1) * 128], wo.ap[:, c, hh * 512:(hh + 1) * 512]
                            S.op("pe", lambda e, o=o, l=l, r=r, st=(c == 0), sp=(c == 7):
                                 e.matmul(o, l, r, start=st, stop=sp), [(wo.key, c), ("gbf", c)], [pb.key])
                        xh = xs_t[:, hh * 512:(hh + 1) * 512]
                        kb.stt("dve", xh, xh, ALPHA, pb, ALU.mult, ALU.add)
                    ln_rows(xs_t, xs_t, bcs[2], bcs[3], dst_bf=None)
                    kb.copy("act", xnb2, xs_t)
                    for g in range(2):
                        pb = ps()
                        for k in range(4):
                            dc = g * 4 + k
                            kb.mm(pb[:, k * 128:(k + 1) * 128], xnb2[:, dc * 128:(dc + 1) * 128], ident)
                        kb.copy("act" if g == 0 else "dve",
                                T(xT.ap[:, g * 4:(g + 1) * 4, sb_ * 128:(sb_ + 1) * 128], ("xT", sb_)),
                                pb.re("q (k t) -> q k t", k=4))
                for fc in range(NFC):
                    k = cnt_["wu"]
                    cnt_["wu"] += 1
                    pre_wu(k + NWU - 1)
                    wg, wv = wub[k % NWU]
                    pg, pvl = ps(), ps()
                    for dst_, w_ in ((pg, wg), (pvl, wv)):
                        for dc in range(8):
                            o, l, r = dst_.ap[:, 0:N], w_.ap[:, dc, :], xT.ap[:, dc, 0:N]
                            S.op("pe", lambda e, o=o, l=l, r=r, st=(dc == 0), sp=(dc == 7):
                                 e.matmul(o, l, r, start=st, stop=sp), [w_.key] + xTk, [dst_.key])
                    G = Gb[fc % 2]
                    c_ = ct[0][:, 0:N]
                    kb.copy("pool", G[:, 0:2], Ghalo[:, fc, :])
                    kb.copy("act", G[:, 2:N + 2], pg[:, 0:N])
                    kb.act(c_, pg[:, 0:N], AF.Identity, bias=pv("cb", fc), scale=pv("cw2", fc))
                    kb.stt("dve", c_, G[:, 1:N + 1], pv("cw1", fc), c_, ALU.mult, ALU.add)
                    kb.stt("dve", c_, G[:, 0:N], pv("cw0", fc), c_, ALU.mult, ALU.add)
                    kb.copy("pool", Ghalo[:, fc, :], G[:, N:N + 2])
                    if is_halo:
                        kb.ts("pool", Ghalo[:, fc, :], Ghalo[:, fc, :], cmask[:, 0:1], None, ALU.mult)
                        continue
                    kb.act(c_, c_, AF.Gelu)
                    kb.tt("dve", T(actb.ap[:, fc, 0:N], ("actb", fc)), pvl[:, 0:N], c_, ALU.mult)
                if is_halo:
                    return
                pds = [[ps() for hh in range(2)] for sb_ in range(nsub)]
                for fc in range(NFC):
                    k = cnt_["wd"]
                    cnt_["wd"] += 1
                    pre_wd(k + NWD - 1)
                    wd_ = wdb[k % NWD]
                    for sb_ in range(nsub):
                        for hh in range(2):
                            kb.mm(pds[sb_][hh], T(actb.ap[:, fc, sb_ * 128:(sb_ + 1) * 128], ("actb", fc)),
                                  wd_[:, hh * 512:(hh + 1) * 512], start=(fc == 0), stop=(fc == NFC - 1))
                for sb_ in range(nsub):
                    xs_t = T(xt.ap[:, sb_, :], ("xt", sb_))
                    for hh in range(2):
                        xh = xs_t[:, hh * 512:(hh + 1) * 512]
                        kb.stt("dve", xh, xh, ALPHA, pds[sb_][hh], ALU.mult, ALU.add)
                    ln_rows(xs_t, xs_t, bcs[4], bcs[5], dst_bf=None)
                    row = (c0 - 2 + sb_) * 128
                    kb.dma("sp", T(out_d[row:row + 128, :], ("out", row)), xs_t, ("st_out", sb_))

            for (c0_, ns_, ih_) in tiles_B:
                phaseB_tile(c0_, ns_, ih_)

        S.barrier()
        S.q["sp"].append((S.pending_barrier["sp"], None, None))

        S.alloc_sems()
        with nc.Block() as block:
            @block.tensor
            def _(e):
                S.replay("pe", e)

            @block.scalar
            def _(e):
                S.replay("act", e)

            @block.vector
            def _(e):
                S.replay("dve", e)

            @block.gpsimd
            def _(e):
                S.replay("pool", e)

            @block.sync
            def _(e):
                S.replay("sp", e)
    return nc


def _consts():
    s = np.arange(128)
    strict = (s[:, None] < s[None, :]).astype(np.float32)
    incl = (s[:, None] <= s[None, :]).astype(np.float32)
    cf = np.zeros((128, 1280), np.float32)
    cf[:, 0:128] = strict
    cf[:, 128:256] = incl
    cf[:, 256:384] = strict
    cf[:, 384:512] = incl
    for q in range(4):
        cf[:, 512 + q * 128:512 + (q + 1) * 128] = strict.T
    e = np.float32(-np.exp(-0.5))
    cf[:, 1024:1152] = incl * e
    cf[:, 1152:1280] = strict * e
    cb = np.zeros((128, 768), np.float32)
    for q in range(4):
        cb[:, q * 128:(q + 1) * 128] = np.eye(128)
    blk = np.zeros((128, 128), np.float32)
    blk[0:64, 0:64] = 1.0
    blk[64:128, 64:128] = 1.0
    cb[:, 512:640] = blk
    ch = s // 64
    cb[:, 640:768] = (ch[None, :] >= ch[:, None]).astype(np.float32)
    return cf, cb.astype(ml_dtypes.bfloat16)


def _chunkcols(v, n):
    v = np.asarray(v, np.float32).reshape(-1)
    pad = np.zeros(n * 128, np.float32)
    pad[:v.size] = v
    return pad.reshape(n, 128).T


def prep_inputs(inp):
    f = lambda a: np.ascontiguousarray(np.asarray(a, np.float32))
    x = f(inp["x"])
    cf, cb = _consts()
    pvec = np.zeros((128, NPV), np.float32)

    def put(name, v, n):
        pvec[:, PV[name]:PV[name] + n] = _chunkcols(v, n)
    put("mu", inp["mu_shift"][0], 27)
    put("bg", inp["b_gate"][0], 16)
    put("sg", inp["sgu_ln_g"][0], 8)
    put("sb", inp["sgu_ln_b"][0], 8)
    put("a0", inp["a0"][0], 8)
    put("kk", inp["k_k"][0], 8)
    put("ka", inp["k_a"][0], 8)
    put("rk", f(inp["r_k"][0]).reshape(-1), 8)
    put("lxg", inp["lnx_g"][0], 8)
    put("lxb", inp["lnx_b"][0], 8)
    put("lig", inp["ln_in_g"], 8)
    put("lib", inp["ln_in_b"], 8)
    cw = f(inp["conv_w"][0])
    put("cw0", cw[0], 21)
    put("cw1", cw[1], 21)
    put("cw2", cw[2], 21)
    put("cb", inp["conv_b"][0], 21)
    w2aug = np.concatenate([f(inp["w2"][0]), f(inp["w0"][0]).reshape(1, D)], 0)
    bc = np.stack([np.broadcast_to(f(v).reshape(1, D), (128, D)) for v in
                   (inp["ln_in_g"], inp["ln_in_b"], inp["ln1_g"][0], inp["ln1_b"][0],
                    inp["ln2_g"][0], inp["ln2_b"][0])], 0)
    shared = {
        "pvec": pvec, "w_in": f(inp["w_in"][0]), "w_o": f(inp["w_o"][0]), "w_up": f(inp["w_up"][0]),
        "w_down": f(inp["w_down"][0]), "w2aug": f(w2aug), "a2": f(inp["a2"][0]), "g2": f(inp["g2"][0]),
        "w_sT": f(np.transpose(f(inp["w_s"][0]), (0, 2, 1))), "b_s": f(inp["b_s"][0]),
        "sgb_row": f(inp["sgu_ln_b"][0]).reshape(8, 128), "bcast": f(bc), "cf32": cf, "cbf16": cb,
    }
    maps = []
    for c in range(NCORES):
        b, q = c // 4, c % 4
        t0 = q * OWN
        xs = np.zeros((ROWS, D), np.float32)
        lo = t0 - 256
        if lo >= 0:
            xs[:] = x[b, lo:t0 + OWN]
        else:
            xs[256:] = x[b, 0:OWN]
        cm = np.zeros((128, 16), np.float32)
        cm[:, 0] = 0.0 if q == 0 else 1.0
        if q > 0:
            cm[:, 1 + (b * 3 + q - 1)] = 1.0
        m = dict(shared)
        m["xs"] = xs
        m["cmask"] = cm
        maps.append(m)
    return maps


_NC_CACHE = {}


def kernel(**inputs):
    if "nc" not in _NC_CACHE:
        _NC_CACHE["nc"] = build_program(False)
    nc = _NC_CACHE["nc"]
    maps = prep_inputs(inputs)
    res = run_bass_kernel_spmd(nc, maps, core_ids=list(range(NCORES)))
    out = np.zeros((2, SEQ, D), np.float32)
    for c in range(NCORES):
        b, q = c // 4, c % 4
        out[b, q * OWN:(q + 1) * OWN] = res.results[c]["out"]
    return out
```

```python
from contextlib import ExitStack
import os
import numpy as np
import ml_dtypes
import concourse.bass as bass
import concourse.mybir as mybir
from concourse.bass_utils import run_bass_kernel_spmd

F32 = mybir.dt.float32
BF16 = mybir.dt.bfloat16
AF = mybir.ActivationFunctionType
ALU = mybir.AluOpType
AX = mybir.AxisListType

NCORES = 8
D = 1024
SEQ = 16384
OWN = 4096
NCH = 34
ROWS = NCH * 128
NTOK = 33 * 128
RKW = 3360
INC = 7456
DFF = 2688
NFC = 21
LN_EPS = 1e-5
GN_EPS = 64e-5
ALPHA = 2.0 ** 0.25
SEMCH = 6000
BIS = int(os.environ.get('BIS', '0'))

PV = {}
_o = 0
for _n, _w in [("mu", 27), ("bg", 16), ("sg", 8), ("sb", 8), ("a0", 8), ("kk", 8), ("ka", 8),
               ("rk", 8), ("lxg", 8), ("lxb", 8), ("lig", 8), ("lib", 8), ("cw0", 21), ("cw1", 21),
               ("cw2", 21), ("cb", 21)]:
    PV[_n] = _o
    _o += _w
NPV = _o


class T:
    __slots__ = ("ap", "key")

    def __init__(self, ap, key):
        self.ap = ap
        self.key = key

    def __getitem__(self, idx):
        return T(self.ap[idx], self.key)

    def re(self, s, **kw):
        return T(self.ap.rearrange(s, **kw), self.key)


class Sched:
    ENGS = ("pe", "act", "dve", "pool", "sp")

    def __init__(self, nc, es):
        self.nc = nc
        self.es = es
        self.q = {e: [] for e in self.ENGS}
        self.cnt = {e: 0 for e in self.ENGS}
        self.bufs = {}
        self.slot_cnt = {}
        self.slot_sem = {}
        self.eng_sems = {e: [] for e in self.ENGS}
        self.pending_barrier = {e: None for e in self.ENGS}

    def _deps(self, reads, writes):
        deps = set()
        for k in reads:
            b = self.bufs.get(k)
            if b and b[0] is not None:
                deps.add(b[0])
        for k in writes:
            b = self.bufs.get(k)
            if b:
                if b[0] is not None:
                    deps.add(b[0])
                deps.update(b[1])
        return deps

    def _update(self, tok, reads, writes):
        for k in reads:
            if k in writes:
                continue
            b = self.bufs.setdefault(k, [None, []])
            b[1].append(tok)
            if len(b[1]) > 64:
                b[1] = self._prune(b[1])
        for k in writes:
            self.bufs[k] = [tok, []]

    @staticmethod
    def _prune(toks):
        best = {}
        for t in toks:
            kk = (t[0], t[1])
            if kk not in best or best[kk][2] < t[2]:
                best[kk] = t
        return list(best.values())

    def op(self, eng, fn, reads, writes):
        reads = [r for r in reads if r is not None]
        deps = self._deps(reads, writes)
        if self.pending_barrier[eng] is not None:
            deps |= self.pending_barrier[eng]
            self.pending_barrier[eng] = None
        self.cnt[eng] += 1
        tok = ("e", eng, self.cnt[eng])
        self._update(tok, reads, writes)
        self.q[eng].append((deps, fn, tok))
        return tok

    def dma(self, eng, out, in_, slot, serialize=True):
        reads = [in_.key] if in_.key is not None else []
        writes = [out.key] if out.key is not None else []
        deps = self._deps(reads, writes)
        if self.pending_barrier[eng] is not None:
            deps |= self.pending_barrier[eng]
            self.pending_barrier[eng] = None
        n = self.slot_cnt.get(slot, 0) + 1
        self.slot_cnt[slot] = n
        if n > 1 and serialize:
            deps.add(("d", slot, n - 1))
        tok = ("d", slot, n)
        self._update(tok, reads, writes)
        oa, ia = out.ap, in_.ap
        self.q[eng].append((deps, lambda e: e.dma_start(out=oa, in_=ia), tok))
        return tok

    def barrier(self):
        toks = set()
        for e in self.ENGS:
            if self.cnt[e] > 0:
                toks.add(("e", e, self.cnt[e]))
        for s, n in self.slot_cnt.items():
            toks.add(("d", s, n))
        for e in self.ENGS:
            self.pending_barrier[e] = set(toks) | (self.pending_barrier[e] or set())

    def alloc_sems(self):
        for e in self.ENGS:
            if e == "sp":
                continue
            nch = (self.cnt[e] + SEMCH - 1) // SEMCH
            for i in range(max(nch, 1)):
                self.eng_sems[e].append(self.es.enter_context(self.nc.semaphore(f"s_{e}{i}")))
        for s in self.slot_cnt:
            self.slot_sem[s] = self.es.enter_context(self.nc.semaphore("d_" + str(s).replace(" ", "")))

    def tok_sem(self, tok):
        if tok[0] == "e":
            idx = tok[2] - 1
            return self.eng_sems[tok[1]][idx // SEMCH], idx % SEMCH + 1
        return self.slot_sem[tok[1]], 16 * tok[2]

    def replay(self, eng, engine):
        waited = {}
        for deps, fn, tok in self.q[eng]:
            need = {}
            for d in deps:
                if d[0] == "e" and d[1] == eng and eng == "pe":
                    continue
                if d[0] == "e" and d[1] == eng and eng == "sp":
                    continue
                sem, val = self.tok_sem(d)
                sid = id(sem)
                if need.get(sid, (None, 0))[1] < val:
                    need[sid] = (sem, val)
            for sid, (sem, val) in need.items():
                if waited.get(sid, 0) >= val:
                    continue
                engine.wait_ge(sem, val)
                waited[sid] = val
            if fn is None:
                continue
            ins = fn(engine)
            sem, val = self.tok_sem(tok)
            if tok[0] == "e":
                ins.then_inc(sem, 1)
            else:
                ins.then_inc(sem, 16)


class KB:
    def __init__(self, nc, es, debug=False):
        self.nc = nc
        self.es = es
        self.S = Sched(nc, es)
        self.debug = debug
        self.ps_rr = 0
        self.uid = 0

    def sb(self, name, shape, dt, key=None):
        t = self.es.enter_context(self.nc.sbuf_tensor(name, list(shape), dt))
        return T(t[:] if False else t.ap() if hasattr(t, "ap") else t[:], key or name)

    def dram(self, name, shape, dt, kind="Internal"):
        return self.nc.dram_tensor(name, list(shape), dt, kind=kind).ap()

    def mm(self, out, lhsT, rhs, start=True, stop=True):
        o, l, r = out.ap, lhsT.ap, rhs.ap
        self.S.op("pe", lambda e: e.matmul(o, l, r, start=start, stop=stop),
                  [lhsT.key, rhs.key], [out.key])

    def act(self, out, in_, func, bias=None, scale=1.0, eng="act"):
        o, i = out.ap, in_.ap
        b = bias.ap if isinstance(bias, T) else bias
        s = scale.ap if isinstance(scale, T) else scale
        reads = [in_.key]
        if isinstance(bias, T):
            reads.append(bias.key)
        if isinstance(scale, T):
            reads.append(scale.key)
        if b is None:
            self.S.op("act", lambda e: e.activation(o, i, func, scale=s), reads, [out.key])
        else:
            self.S.op("act", lambda e: e.activation(o, i, func, bias=b, scale=s), reads, [out.key])

    def tt(self, eng, out, in0, in1, op):
        o, a, b = out.ap, in0.ap, in1.ap
        self.S.op(eng, lambda e: e.tensor_tensor(o, a, b, op), [in0.key, in1.key], [out.key])

    def ts(self, eng, out, in0, s1, s2, op0, op1=None):
        o, a = out.ap, in0.ap
        reads = [in0.key]
        v1 = s1.ap if isinstance(s1, T) else s1
        v2 = s2.ap if isinstance(s2, T) else s2
        if isinstance(s1, T):
            reads.append(s1.key)
        if isinstance(s2, T):
            reads.append(s2.key)
        if op1 is None:
            self.S.op(eng, lambda e: e.tensor_scalar(o, a, v1, None, op0), reads, [out.key])
        else:
            self.S.op(eng, lambda e: e.tensor_scalar(o, a, v1, v2, op0, op1), reads, [out.key])

    def stt(self, eng, out, in0, sc, in1, op0, op1):
        o, a, b = out.ap, in0.ap, in1.ap
        reads = [in0.key, in1.key]
        v = sc.ap if isinstance(sc, T) else sc
        if isinstance(sc, T):
            reads.append(sc.key)
        self.S.op(eng, lambda e: e.scalar_tensor_tensor(o, a, v, b, op0, op1), reads, [out.key])

    def copy(self, eng, out, in_):
        o, i = out.ap, in_.ap
        if eng == "act":
            self.S.op("act", lambda e: e.activation(o, i, AF.Copy), [in_.key], [out.key])
        else:
            self.S.op(eng, lambda e: e.tensor_copy(o, i), [in_.key], [out.key])

    def memset(self, eng, out, val):
        o = out.ap
        self.S.op(eng, lambda e: e.memset(o, val), [], [out.key])

    def dma(self, eng, out, in_, slot, serialize=True):
        self.S.dma(eng, out, in_, slot, serialize)


def _dram_T(ap, key=None):
    return T(ap, key)


def build_program(debug=False, ntiles=NCH // 2, do_scan=True, do_cc=True, nbt=8):
    nc = bass.Bass("TRN2", target_bir_lowering=False)
    es = ExitStack()
    with es:
        kb = KB(nc, es, debug)
        S = kb.S
        _qs = ("sp", "act", "pool")
        _qi = [0]

        _qmap = {}

        def qfor(slot):
            if slot not in _qmap:
                _qi[0] += 1
                _qmap[slot] = _qs[_qi[0] % 3]
            return _qmap[slot]
        def din(name, shape, dt=F32):
            return nc.dram_tensor(name, list(shape), dt, kind="ExternalInput").ap()

        xs = din("xs", [ROWS, D])
        cmask_d = din("cmask", [128, 16])
        pvec_d = din("pvec", [128, NPV])
        w_in_d = din("w_in", [D, INC])
        if not debug:
            w_o_d = din("w_o", [D, D])
            w_up_d = din("w_up", [D, 2 * DFF])
            w_down_d = din("w_down", [DFF, D])
        w2_d = din("w2aug", [65, D])
        a2_d = din("a2", [64, D])
        g2_d = din("g2", [160, D])
        if not debug:
            wsT_d = din("w_sT", [8, 128, 128])
            bs_d = din("b_s", [8, 128])
            sgb_row_d = din("sgb_row", [8, 128])
            bc_d = din("bcast", [6, 128, D])
        cf_d = din("cf32", [128, 10 * 128])
        cb_d = din("cbf16", [128, 8 * 128], BF16)
        out_d = nc.dram_tensor("out", [OWN, D], F32, kind="ExternalOutput").ap()

        ydram = nc.dram_tensor("ydram", [D, NTOK], F32, kind="Internal").ap()
        rdram = nc.dram_tensor("rdram", [D, NTOK], BF16, kind="Internal").ap()
        bdram = nc.dram_tensor("bdram", [D, NTOK], BF16, kind="Internal").ap()
        gdram = nc.dram_tensor("gdram", [D, NTOK], BF16, kind="Internal").ap()
        wBd = nc.dram_tensor("wB_bf", [D, 4096], BF16, kind="Internal").ap()
        wod = nc.dram_tensor("wo_bf", [D, D], BF16, kind="Internal").ap()
        wud = nc.dram_tensor("wu_bf", [D, 2 * DFF], BF16, kind="Internal").ap()
        wdd = nc.dram_tensor("wd_bf", [DFF, D], BF16, kind="Internal").ap()
        wAd = nc.dram_tensor("wA_bf", [D, RKW], BF16, kind="Internal").ap()
        cc_in = nc.dram_tensor("cc_in", [1024, 128], F32, kind="Internal").ap()
        cc_out = nc.dram_tensor("cc_out", [NCORES * 1024, 128], F32, kind="Internal").ap()
        if debug:
            dbg_y = nc.dram_tensor("dbg_y", [D, NTOK], F32, kind="ExternalOutput").ap()
            dbg_r = nc.dram_tensor("dbg_r", [D, NTOK], BF16, kind="ExternalOutput").ap()
            dbg_b = nc.dram_tensor("dbg_b", [D, NTOK], BF16, kind="ExternalOutput").ap()
            dbg_g = nc.dram_tensor("dbg_g", [D, NTOK], BF16, kind="ExternalOutput").ap()
            dbg_s = nc.dram_tensor("dbg_s", [1024, 128], F32, kind="ExternalOutput").ap()

        psb = []
        for b in range(8):
            t = es.enter_context(nc.psum_tensor(f"ps{b}", [128, 512], F32))
            psb.append(T(t[:], ("ps", b)))

        def ps():
            b = psb[kb.ps_rr % 8]
            kb.ps_rr += 1
            return b

        def sbt(name, shape, dt, key=None):
            t = es.enter_context(nc.sbuf_tensor(name, list(shape), dt))
            return T(t[:], key or name)

        cf = sbt("cf", [128, 10 * 128], F32)
        cbf = sbt("cbf", [128, 8 * 128], BF16)
        pvec = sbt("pvec_s", [128, NPV], F32)
        omm = sbt("omm", [128, 27], F32)
        cmask = sbt("cmask_s", [128, 16], F32)
        maskM = cf[:, 0:512]
        maskT4 = cf[:, 512:1024]
        tri_i = cf[:, 1024:1152]
        tri_x = cf[:, 1152:1280]
        ident = cbf[:, 0:128]
        ident4 = cbf[:, 0:512]
        blkb = cbf[:, 512:640]
        sgumask = cbf[:, 640:768]
        tri_ib = cbf[:, 768:896]
        tri_xb = cbf[:, 896:1024]
        CDEC = float(np.exp(-0.5))

        def pv(name, c=0, rows=128):
            return pvec[0:rows, PV[name] + c:PV[name] + c + 1]

        kb.dma("sp", cf, T(cf_d, None), "c0")
        kb.dma("sp", cbf, T(cb_d, None), "c1")
        kb.dma("sp", pvec, T(pvec_d, None), "c2")
        kb.dma("sp", cmask, T(cmask_d, None), "c3")
        kb.ts("dve", omm, pvec[:, PV["mu"]:PV["mu"] + 27], -1.0, 1.0, ALU.mult, ALU.add)
        epsc = sbt("epsc", [128, 4], F32)
        kb.memset("dve", epsc[:, 0:1], LN_EPS)
        kb.memset("dve", epsc[:, 1:2], 1e-24)
        kb.memset("dve", epsc[:, 2:3], GN_EPS)
        blkf = sbt("blkf", [128, 128], F32)
        kb.copy("dve", blkf, blkb)

        WKA = ("wbfA",)
        for dc in range(8):
            rs_ = slice(dc * 128, (dc + 1) * 128)
            kb.dma("pool", T(wAd[rs_, :], WKA), T(w_in_d[rs_, 2048:2048 + RKW], None), "wconvA", serialize=False)
        if not debug:
            WK = ("wbf",)
            for dc in range(8):
                rs_ = slice(dc * 128, (dc + 1) * 128)
                kb.dma("pool", T(wBd[rs_, 0:2048], WK), T(w_in_d[rs_, 0:2048], None), "wconv", serialize=False)
                kb.dma("pool", T(wBd[rs_, 2048:4096], WK), T(w_in_d[rs_, 5408:7456], None), "wconv", serialize=False)
                kb.dma("pool", T(wud[rs_, :], WK), T(w_up_d[rs_, :], None), "wconv", serialize=False)
                kb.dma("pool", T(wod[rs_, :], WK), T(w_o_d[rs_, :], None), "wconv", serialize=False)
            for fc in range(NFC):
                rs_ = slice(fc * 128, (fc + 1) * 128)
                kb.dma("pool", T(wdd[rs_, :], WK), T(w_down_d[rs_, :], None), "wconv", serialize=False)
        esA = ExitStack()
        es.enter_context(esA)

        def sba(name, shape, dt, key=None):
            t = esA.enter_context(nc.sbuf_tensor(name, list(shape), dt))
            return T(t[:], key or name)

        NA = 256
        wAh = sba("wAh", [128, 8, 288], BF16)
        wAp = [sba(f"wAp{i}", [128, 8, 3, 128], BF16) for i in range(2)]
        wAdv = wAd.rearrange("(dc q) c -> q dc c", q=128)
        wAd3 = wAd[:, 0:3072].rearrange("(dc q) (j pp c) -> q dc j pp c", q=128, j=3, pp=8)
        stA = {"n": 0}

        def pre_wA(k):
            while stA["n"] <= min(k, 9 * ntiles - 1):
                i = stA["n"]
                j = i % 9
                if j == 0:
                    kb.dma(qfor("wAh"), wAh, T(wAdv[:, :, 3072:3360], WKA), "wAh")
                else:
                    buf = wAp[(j - 1) % 2]
                    for j3 in range(3):
                        kb.dma(qfor(("wAp", (j - 1) % 2, j3)), T(buf.ap[:, :, j3, :], (buf.key, j3)), T(wAd3[:, :, j3, j - 1, :], WKA),
                               ("wAp", (j - 1) % 2, j3))
                stA["n"] += 1
        w2b = sba("w2b", [65, D], BF16)
        kb.dma("pool", w2b, T(w2_d, None), "w2b")
        a2b = sba("a2b", [128, D], BF16)
        kb.dma("pool", a2b[64:128, :], T(a2_d, None), "a2b")
        g2b0 = sba("g2b0", [128, D], BF16)
        kb.dma("pool", g2b0, T(g2_d[0:128, :], None), "g2b0")
        g2b1 = sba("g2b1", [32, D], BF16)
        kb.dma("pool", g2b1, T(g2_d[128:160, :], None), "g2b1")

        xin = [sba(f"xin{i}", [128, D], F32) for i in range(1)]
        xnb = [sba(f"xnb{i}", [128, D], BF16) for i in range(1)]
        stats = sba("stats", [128, 2, 6], F32)
        mv = sba("mv", [128, 2], F32)
        rstd = sba("rstd", [128, 1], F32)
        omka = sba("omka", [128, 8], F32)
        kb.ts("dve", omka, pvec[:, PV["ka"]:PV["ka"] + 8], -1.0, 1.0, ALU.mult, ALU.add)
        hT = sba("hT", [128, 8, NA], BF16)
        prevc = sba("prevc", [128, 27], F32)
        kb.memset("pool", prevc, 0.0)
        tmpl = [sba(f"tmpl{i}", [128, NA], F32) for i in range(1)]
        wdad = sba("wdad", [128, NA], F32)
        gd0 = sba("gd0", [128, NA], F32)
        gd1 = sba("gd1", [32, NA], F32)
        thw = sba("thw", [65, NA], BF16)
        kb.memset("pool", thw, 1.0)
        sgd0 = sba("sgd0", [128, NA], BF16)
        sgd1 = sba("sgd1", [32, NA], BF16)
        adb = sba("adb", [128, NA], BF16)
        lwT = sba("lwT", [128, 2, D], BF16)
        t1b = sba("t1b", [128, NA], BF16)
        t3b = sba("t3b", [128, NA], BF16)
        NT_ = 10
        ptmp = [[sba(f"pt{i}_{j}", [128, NA], F32) for j in range(NT_)] for i in range(1)]
        gout = [sba(f"gout{i}", [128, NA], BF16) for i in range(2)]
        bout = [sba(f"bout{i}", [128, NA], BF16) for i in range(2)]
        gLs2 = [sba(f"gLs{i}", [128, 8, 2], F32) for i in range(2)]
        ARt2 = [sba(f"ARt{i}", [128, 8, 2, 2, 128], BF16) for i in range(2)]
        Btt2 = [sba(f"Btt{i}", [128, 8, 2, 128], BF16) for i in range(2)]
        Ktt2 = [sba(f"Ktt{i}", [128, 8, 2, 128], BF16) for i in range(2)]
        Kht2 = [sba(f"Kht{i}", [128, 8, 2, 128], BF16) for i in range(2)]
        Bht2 = [sba(f"Bht{i}", [128, 8, 2, 128], BF16) for i in range(2)]
        Vtt2 = [sba(f"Vtt{i}", [128, 8, 2, 128], BF16) for i in range(2)]
        VaT = [sba(f"VaT{h}", [128, 8, 128], BF16) for h in range(2)]
        KhTz = [sba(f"KhTz{h}", [128, 8, 128], BF16) for h in range(2)]
        BhTz = [sba(f"BhTz{h}", [128, 8, 128], BF16) for h in range(2)]
        M3 = [sba(f"M3{h}", [128, 8, 384], BF16) for h in range(2)]
        Mp = [[sba(f"Mp{h}{q}", [128, 8, 128], BF16) for q in range(2)] for h in range(2)]
        MpT = [[sba(f"MpT{h}{q}", [128, 8, 128], BF16) for q in range(2)] for h in range(2)]
        Xb = [[sba(f"Xb{h}{q}", [128, 8, 128], BF16) for q in range(2)] for h in range(2)]
        WTb = [sba(f"WTb{h}", [128, 8, 128], BF16) for h in range(2)]
        UTb = [sba(f"UTb{h}", [128, 8, 128], BF16) for h in range(2)]
        Yst = [sba(f"Yst{h}", [64, 8, 128], F32) for h in range(2)]
        Rst = [sba(f"Rst{h}", [128, 8, 128], BF16) for h in range(2)]
        STb = [sba(f"STb{h}", [128, 4, 128], BF16) for h in range(2)]
        STf = [sba(f"STf{h}", [128, 4, 128], F32) for h in range(2)]
        for h in range(2):
            kb.memset("pool", VaT[h], 0.0)
            kb.memset("pool", KhTz[h], 0.0)
            kb.memset("pool", BhTz[h], 0.0)
            kb.memset("pool", STf[h], 0.0)
        for h in range(2):
            for p in range(4):
                kb.copy("pool", STf[h][0:64, p, 64:128], ident[0:64, 0:64])
                kb.copy("pool", STf[h][64:128, p, 64:128], ident[64:128, 64:128])
            kb.copy("pool", STb[h], STf[h])

        def phaseA_tile(tt):
            c0 = 2 * tt
            par = tt % 2
            ARt, Btt, Ktt, Kht, Bht, Vtt, gLs = ARt2[par], Btt2[par], Ktt2[par], Kht2[par], Bht2[par], Vtt2[par], gLs2[par]
            pre_wA(9 * tt + 1)
            for sc in range(2):
                cc = c0 + sc
                xi = xin[0]
                xn = xnb[0]
                kb.dma("sp", xi, T(xs[cc * 128:(cc + 1) * 128, :], None), ("xin", 0))
                for hh in range(2):
                    o, i = stats.ap[:, hh, :], xi.ap[:, hh * 512:(hh + 1) * 512]
                    S.op("dve", lambda e, o=o, i=i: e.bn_stats(o, i), [xi.key], [stats.key])
                o, i = mv.ap, stats.ap
                S.op("dve", lambda e, o=o, i=i: e.bn_aggr(o, i), [stats.key], [mv.key])
                kb.act(rstd, mv[:, 1:2], AF.Ln, bias=epsc[:, 0:1])
                kb.act(rstd, rstd, AF.Exp, scale=-0.5)
                kb.ts("dve", xn, xi, mv[:, 0:1], rstd, ALU.subtract, ALU.mult)
                for g in range(2):
                    pb = ps()
                    for k in range(4):
                        dc = g * 4 + k
                        kb.mm(pb[:, k * 128:(k + 1) * 128], xn[:, dc * 128:(dc + 1) * 128], ident)
                    for k in range(4):
                        dc = g * 4 + k
                        kb.act(T(hT.ap[:, dc, sc * 128:(sc + 1) * 128], ("hT", sc)),
                               pb[:, k * 128:(k + 1) * 128], AF.Identity,
                               bias=pv("lib", dc), scale=pv("lig", dc))
                    yield

            def proj(fc, dst, wbuf, wl, rows=128):
                pb = ps()
                for dc in range(8):
                    o = pb[0:rows, 0:NA]
                    kb.S.op("pe", lambda e, o=o.ap, l=wl(dc), r=hT.ap[:, dc, :], st=(dc == 0), sp=(dc == 7):
                            e.matmul(o, l, r, start=st, stop=sp),
                            [wbuf, ("hT", 0), ("hT", 1)], [pb.key])
                tm = tmpl[0]
                kb.act(tm[0:rows, :], pb[0:rows, 0:NA], AF.Copy, scale=omm[0:rows, fc:fc + 1])
                kb.stt("dve", dst[0:rows, 1:NA], pb[0:rows, 0:NA - 1], pv("mu", fc, rows), tm[0:rows, 1:NA],
                       ALU.mult, ALU.add)
                kb.stt("dve", dst[0:rows, 0:1], prevc[0:rows, fc:fc + 1], pv("mu", fc, rows), tm[0:rows, 0:1],
                       ALU.mult, ALU.add)
                kb.copy("dve", prevc[0:rows, fc:fc + 1], pb[0:rows, NA - 1:NA])
                if tt == 0:
                    kb.ts("dve", prevc[0:rows, fc:fc + 1], prevc[0:rows, fc:fc + 1], cmask[0:rows, 0:1], None,
                          ALU.mult)

            proj(24, wdad, wAh.key, lambda dc: wAh.ap[:, dc, 0:128])
            proj(25, gd0, wAh.key, lambda dc: wAh.ap[:, dc, 128:256])
            proj(26, gd1, wAh.key, lambda dc: wAh.ap[:, dc, 256:288], rows=32)
            yield
            kb.act(thw[0:64, :], wdad[0:64, :], AF.Tanh)
            kb.copy("dve", adb[64:128, :], wdad[64:128, :])
            kb.act(sgd0, gd0, AF.Sigmoid)
            kb.act(sgd1, gd1, AF.Sigmoid)
            for sc in range(2):
                for hh in range(2):
                    pb = ps()
                    kb.mm(pb, thw[0:65, sc * 128:(sc + 1) * 128], w2b[0:65, hh * 512:(hh + 1) * 512])
                    kb.act(T(lwT.ap[:, sc, hh * 512:(hh + 1) * 512], ("lwT", sc)), pb, AF.Sigmoid)
                yield
            for p in range(8):
                pre_wA(9 * tt + 1 + p + 1)
                wb = wAp[p % 2]
                pt = ptmp[0]
                a_, r_, k_, ep, em, ex, kkt, t1, t2, t3 = pt
                cs = slice(p * 128, (p + 1) * 128)
                pb = ps()
                kb.mm(pb[:, 0:NA], a2b[64:128, cs], adb[64:128, :])
                kb.act(a_, pb[:, 0:NA], AF.Sigmoid, bias=pv("a0", p))
                pb = ps()
                kb.mm(pb[:, 0:NA], g2b0[:, cs], sgd0, start=True, stop=False)
                kb.mm(pb[:, 0:NA], g2b1[0:32, cs], sgd1[0:32, :], start=False, stop=True)
                go = gout[p % 2]
                kb.copy("act", go, pb[:, 0:NA])
                for sc in range(2):
                    cc = c0 + sc
                    if cc >= 1:
                        kb.dma("pool", T(gdram[cs, (cc - 1) * 128:cc * 128], ("gdram", p, cc)),
                               go[:, sc * 128:(sc + 1) * 128], ("gout", p % 2, sc))
                pb = ps()
                for sc in range(2):
                    lw_ = T(lwT.ap[:, sc, cs], ("lwT", sc))
                    kb.mm(pb[:, sc * 128:(sc + 1) * 128], lw_, tri_ib)
                    kb.mm(pb[:, 256 + sc * 128:256 + (sc + 1) * 128], lw_, tri_xb)
                kb.act(ep, pb[:, 0:NA], AF.Exp, scale=-CDEC)
                kb.act(em, pb[:, 0:NA], AF.Exp, scale=CDEC)
                kb.act(ex, pb[:, 256:256 + NA], AF.Exp, scale=-CDEC)
                for sc in range(2):
                    kb.copy("pool", T(gLs.ap[:, p, sc:sc + 1], ("gLs", par, p)), ep[:, sc * 128 + 127:sc * 128 + 128])
                yield
                vb = T(Vtt.ap[:, p].rearrange("q c t -> q (c t)"), ("vt", par, p))
                proj(p, r_, (wb.key, 0), lambda dc: wb.ap[:, dc, 0, :])
                proj(8 + p, k_, (wb.key, 1), lambda dc: wb.ap[:, dc, 1, :])
                proj(16 + p, vb, (wb.key, 2), lambda dc: wb.ap[:, dc, 2, :])
                if tt == 0:
                    kb.ts("dve", vb[:, 128:256], vb[:, 128:256], cmask[:, 0:1], None, ALU.mult)
                yield
                kb.act(kkt, k_, AF.Copy, scale=pv("kk", p))
                kb.act(t1b, k_, AF.Square, scale=pv("kk", p))
                pb = ps()
                kb.mm(pb[:, 0:NA], blkb, t1b)
                kb.act(t2, pb[:, 0:NA], AF.Ln, bias=epsc[:, 1:2])
                kb.act(t2, t2, AF.Exp, scale=-0.5)
                kb.tt("dve", kkt, kkt, t2, ALU.mult)
                kb.act(t1, a_, AF.Identity, bias=omka[:, p:p + 1], scale=pv("ka", p))
                kb.tt("pool", t1, t1, k_, ALU.mult)
                kb.stt("dve", t3b, r_, pv("rk", p), t1, ALU.mult, ALU.mult)
                pb = ps()
                kb.mm(pb[:, 0:NA], blkb, t3b)
                bo = bout[p % 2]
                kb.tt("dve", bo, pb[:, 0:NA], vb, ALU.mult)
                for sc in range(2):
                    cc = c0 + sc
                    if cc >= 1:
                        kb.dma("pool", T(bdram[cs, (cc - 1) * 128:cc * 128], ("bdram", p, cc)),
                               bo[:, sc * 128:(sc + 1) * 128], ("bout", p % 2, sc))
                yield
                key = ("scanop", par, p)
                v4 = lambda t_: t_.ap.rearrange("q (c t) -> q c t", c=2)
                kb.tt("pool", t2, kkt, a_, ALU.mult)
                kb.stt("dve", T(ARt.ap[:, p, :, 0, :], key), T(v4(kkt), kkt.key), -1.0, T(v4(ex), ex.key),
                       ALU.mult, ALU.mult)
                kb.tt("pool", T(ARt.ap[:, p, :, 1, :], key), T(v4(r_), r_.key), T(v4(ep), ep.key), ALU.mult)
                kb.tt("dve", T(Btt.ap[:, p], key), T(v4(t2), t2.key), T(v4(em), em.key), ALU.mult)
                kb.tt("pool", T(Ktt.ap[:, p], key), T(v4(t1), t1.key), T(v4(em), em.key), ALU.mult)
                for sc in range(2):
                    gl = T(gLs.ap[:, p, sc:sc + 1], ("gLs", par, p))
                    kb.act(T(Kht.ap[:, p, sc], key), T(Ktt.ap[:, p, sc], key), AF.Copy, scale=gl)
                    kb.act(T(Bht.ap[:, p, sc], key), T(Btt.ap[:, p, sc], key), AF.Copy, scale=gl)
                yield

        def scan_prep(cc, sc, hf, res):
            P0 = hf * 4
            par = (cc // 2) % 2
            ARt, Btt, Ktt, Kht, Bht, Vtt, gLs = ARt2[par], Btt2[par], Ktt2[par], Kht2[par], Bht2[par], Vtt2[par], gLs2[par]
            keys = [("scanop", par, P0 + q) for q in range(4)]
            for src, dst, mode in ((Vtt, VaT[hf], 0), (Kht, KhTz[hf], 1), (Bht, BhTz[hf], 1)) if not (BIS & 8) else ():
                pb = ps()
                for q in range(4):
                    kq = ("vt", par, P0 + q) if mode == 0 else keys[q]
                    kb.mm(pb[:, q * 128:(q + 1) * 128], T(src.ap[:, P0 + q, sc, :], kq), ident)
                pv_ = pb.ap.rearrange("s (q e j) -> s q e j", q=4, e=2)
                dv = dst.ap.rearrange("s (q e) c -> s q e c", e=2)
                if mode == 0:
                    kb.copy("dve", T(dv[:, :, 0, 0:64], dst.key), T(pv_[:, :, 0, :], pb.key))
                    kb.copy("dve", T(dv[:, :, 1, 0:64], dst.key), T(pv_[:, :, 1, :], pb.key))
                else:
                    kb.copy("dve", T(dv[:, :, 0, 0:64], dst.key), T(pv_[:, :, 0, :], pb.key))
                    kb.copy("dve", T(dv[:, :, 1, 64:128], dst.key), T(pv_[:, :, 1, :], pb.key))
            yield
            if BIS & 16:
                res[hf] = 0
                return
            for hl in range(8 if not (BIS & 32) else 0):
                p, e = P0 + hl // 2, hl % 2
                rs = slice(e * 64, (e + 1) * 64)
                k_ = ("scanop", par, p)
                pb = ps()
                AR = T(ARt.ap[rs, p, sc].rearrange("j a t -> j (a t)"), k_)
                kb.mm(pb[:, 0:256], T(Btt.ap[rs, p, sc, :], k_), AR)
                kb.mm(pb[:, 256:512], T(Ktt.ap[rs, p, sc, :], k_), AR)
                kb.tt("dve", Mp[hf][0][:, hl, :], pb[:, 0:128], maskM[:, 0:128], ALU.mult)
                kb.tt("dve", M3[hf][:, hl, :], pb[:, 128:512], maskM[:, 128:512], ALU.mult)
                if hl % 2 == 1:
                    yield
            for e in range(2 if not (BIS & 64) else 0):
                pb = ps()
                rs = slice(e * 64, (e + 1) * 64)
                for q in range(4):
                    p = P0 + q
                    k_ = ("scanop", par, p)
                    kb.mm(pb[:, q * 128:(q + 1) * 128], T(ARt.ap[rs, p, sc, 0, :], k_), T(Btt.ap[rs, p, sc, :], k_))
                kb.tt("dve", T(MpT[hf][0].ap.rearrange("s (q e) t -> s q e t", e=2)[:, :, e, :], MpT[hf][0].key),
                      pb.re("s (q t) -> s q t", q=4), T(maskT4.ap.rearrange("s (q t) -> s q t", q=4), maskT4.key), ALU.mult)
                yield
            cur = 0
            for g in range(2 if not (BIS & 128) else 0):
                kb.tt("pool", Xb[hf][0][:, g * 4:(g + 1) * 4, :].re("s h t -> s (h t)"),
                      Mp[hf][0][:, g * 4:(g + 1) * 4, :].re("s h t -> s (h t)"), ident4, ALU.add)
            xcur = 0
            for rd in range(1, 8 if not (BIS & 2) else 1):
                nxt = 1 - cur
                for g in range(2):
                    hs = slice(g * 4, (g + 1) * 4)
                    if rd <= 6:
                        if rd <= 5:
                            pb1 = ps()
                            for q in range(4):
                                hl = g * 4 + q
                                kb.mm(pb1[:, q * 128:(q + 1) * 128], MpT[hf][cur][:, hl, :], Mp[hf][cur][:, hl, :])
                        pb2 = ps()
                        for q in range(4):
                            hl = g * 4 + q
                            kb.mm(pb2[:, q * 128:(q + 1) * 128], Mp[hf][cur][:, hl, :], MpT[hf][cur][:, hl, :])
                    if rd >= 2:
                        pb3 = ps()
                        for q in range(4):
                            hl = g * 4 + q
                            kb.mm(pb3[:, q * 128:(q + 1) * 128], MpT[hf][cur][:, hl, :], Xb[hf][xcur][:, hl, :])
                    if rd <= 5:
                        kb.copy("act", Mp[hf][nxt][:, hs, :].re("s h t -> s (h t)"), pb1)
                    if rd <= 6:
                        kb.copy("act" if rd % 2 else "dve", MpT[hf][nxt][:, hs, :].re("s h t -> s (h t)"), pb2)
                    if rd >= 2:
                        kb.tt("dve", Xb[hf][1 - xcur][:, hs, :].re("s h t -> s (h t)"), pb3,
                              Xb[hf][xcur][:, hs, :].re("s h t -> s (h t)"), ALU.add)
                    yield
                if rd >= 2:
                    xcur = 1 - xcur
                cur = nxt
            res[hf] = xcur

        def scan_serial(cc, sc, hf, xcur):
            P0 = hf * 4
            par = (cc // 2) % 2
            ARt, Btt, Ktt, Kht, Bht, Vtt, gLs = ARt2[par], Btt2[par], Ktt2[par], Kht2[par], Bht2[par], Vtt2[par], gLs2[par]
            Xi = Xb[hf][xcur]
            stb, stf = STb[hf], STf[hf]
            for e in range(2):
                pb = ps()
                rs = slice(e * 64, (e + 1) * 64)
                for q in range(4):
                    hl = 2 * q + e
                    p = P0 + q
                    k_ = ("scanop", par, p)
                    o = pb[:, q * 128:(q + 1) * 128]
                    kb.mm(o, T(ARt.ap[rs, p, sc, 0, :], k_), stb[rs, q, :], start=True, stop=False)
                    kb.mm(o, M3[hf][:, hl, 128:256], VaT[hf][:, hl, :], start=False, stop=True)
                kb.copy("act" if e == 0 else "dve",
                        T(WTb[hf].ap.rearrange("s (q e) t -> s q e t", e=2)[:, :, e, :], WTb[hf].key),
                        pb.re("s (q t) -> s q t", q=4))
                yield
            for g in range(2):
                pb = ps()
                for q in range(4):
                    hl = g * 4 + q
                    kb.mm(pb[:, q * 128:(q + 1) * 128], Xi[:, hl, :], WTb[hf][:, hl, :])
                kb.copy("act" if g == 0 else "dve", UTb[hf][:, g * 4:(g + 1) * 4, :].re("s h t -> s (h t)"), pb)
                yield
            for e in range(2):
                pb = ps()
                rs = slice(e * 64, (e + 1) * 64)
                for q in range(4):
                    hl = 2 * q + e
                    p = P0 + q
                    k_ = ("scanop", par, p)
                    o = pb[:, q * 128:(q + 1) * 128]
                    kb.mm(o, stb[rs, q, :], T(ARt.ap[rs, p, sc, 1, :], k_), start=True, stop=False)
                    kb.mm(o, UTb[hf][:, hl, :], M3[hf][:, hl, 0:128], start=False, stop=False)
                    kb.mm(o, VaT[hf][:, hl, :], M3[hf][:, hl, 256:384], start=False, stop=True)
                kb.copy("act", T(Yst[hf].ap.rearrange("s (q e) t -> s q e t", e=2)[0:64, :, e, :], Yst[hf].key),
                        pb[0:64, :].re("s (q t) -> s q t", q=4))
                kb.copy("dve", T(Rst[hf].ap.rearrange("s (q e) t -> s q e t", e=2)[64:128, :, e, :], Rst[hf].key),
                        pb[64:128, :].re("s (q t) -> s q t", q=4))
            col = (cc - 1) * 128
            yv = ydram[hf * 512:(hf + 1) * 512, col:col + 128].rearrange("(h i) t -> i h t", i=64)
            rv = rdram[hf * 512:(hf + 1) * 512, col:col + 128].rearrange("(h i) t -> i h t", i=64)
            if not (BIS & 4):
                kb.dma("pool", T(yv, ("ydram", hf, cc)), Yst[hf][0:64], ("yst", hf, 0))
                kb.dma("act", T(rv, ("rdram", hf, cc)), Rst[hf][64:128], ("yst", hf, 1))
            yield
            pb = ps()
            for q in range(4):
                o = pb[:, q * 128:(q + 1) * 128]
                for e in range(2):
                    hl = q * 2 + e
                    kb.mm(o, BhTz[hf][:, hl, :], UTb[hf][:, hl, :], start=(e == 0), stop=False)
                    kb.mm(o, KhTz[hf][:, hl, :], VaT[hf][:, hl, :], start=False, stop=(e == 1))
            for q in range(4):
                gl = T(gLs.ap[:, P0 + q, sc:sc + 1], ("gLs", par, P0 + q))
                kb.stt("dve", stf[:, q, :], stf[:, q, :], gl, pb[:, q * 128:(q + 1) * 128], ALU.mult, ALU.add)
            kb.copy("act", stb, stf)
            yield

        def run_il(gens):
            gens = list(gens)
            while gens:
                for g_ in list(gens):
                    try:
                        next(g_)
                    except StopIteration:
                        gens.remove(g_)

        def scan_tile(tt):
            for sc in range(2):
                cc = 2 * tt + sc
                if cc == 0 or not do_scan:
                    continue
                res = {}
                gens = [scan_prep(cc, sc, hf, res) for hf in range(2)]
                while gens:
                    for g_ in list(gens):
                        try:
                            next(g_)
                            yield
                        except StopIteration:
                            gens.remove(g_)
                if not (BIS & 1):
                    gens = [scan_serial(cc, sc, hf, res[hf]) for hf in range(2)]
                    while gens:
                        for g_ in list(gens):
                            try:
                                next(g_)
                                yield
                            except StopIteration:
                                gens.remove(g_)
                if cc == 32:
                    for hf in range(2):
                        kb.dma("sp", T(cc_in[hf * 512:(hf + 1) * 512, :].rearrange("(q p) c -> p q c", p=128), ("cc_in", hf)),
                               STf[hf], ("ccin", hf))

        def run_weighted(ga, gb, wa=2, wb=1):
            alive = [ga is not None, gb is not None]
            while any(alive):
                for idx, (g_, w_) in enumerate(((ga, wa), (gb, wb))):
                    if not alive[idx]:
                        continue
                    for _ in range(w_):
                        try:
                            next(g_)
                        except StopIteration:
                            alive[idx] = False
                            break

        run_weighted(None, phaseA_tile(0))
        for tt in range(ntiles):
            nxt = phaseA_tile(tt + 1) if tt + 1 < ntiles else None
            run_weighted(scan_tile(tt), nxt)
        S.barrier()
        cc_tok_holder = []

        def emit_cc(e):
            return e.collective_compute("AllGather", ALU.bypass, replica_groups=[list(range(NCORES))],
                                        ins=[cc_in[:, :]], outs=[cc_out[:, :]])
        if do_cc:
            S.op("pool", emit_cc, [("cc_in", 0), ("cc_in", 1)], [("cc_out",)])
        else:
            for r_ in range(64):
                kb.dma("sp", T(cc_out[r_ * 128:(r_ + 1) * 128, :], ("cc_out",)), cf[:, 0:128], "ccfake")
        S.barrier()
        if debug:
            ncol = 128 * (2 * ntiles - 1)
            for ci in range(0, ncol if not os.environ.get("NODBG") else 0, 128):
                cs_ = slice(ci, ci + 128)
                if do_scan and not (BIS & 5):
                    kb.dma("sp", T(dbg_y[:, cs_], ("dbg", 0)), T(ydram[:, cs_], None), "dbg0")
                    kb.dma("sp", T(dbg_r[:, cs_], ("dbg", 1)), T(rdram[:, cs_], None), "dbg1")
                kb.dma("sp", T(dbg_b[:, cs_], ("dbg", 2)), T(bdram[:, cs_], None), "dbg2")
                kb.dma("sp", T(dbg_g[:, cs_], ("dbg", 3)), T(gdram[:, cs_], None), "dbg3")
            for hf in range(2):
                kb.dma("sp", T(dbg_s[hf * 512:(hf + 1) * 512, :].rearrange("(q p) c -> p q c", p=128), ("dbg", 4)),
                       STf[hf], ("dbg4", hf))
            S.barrier()

        if not debug:
            esA.close()
            esB = ExitStack()
            es.enter_context(esB)

            def sbb(name, shape, dt, key=None):
                t = esB.enter_context(nc.sbuf_tensor(name, list(shape), dt))
                return T(t[:], key or name)

            NB = 512
            bcs = [sbb(f"bc{i}", [128, D], F32) for i in range(6)]
            for i in range(6):
                kb.dma("sp", bcs[i], T(bc_d[i], None), ("bc", i))
            wsT = sbb("wsT", [128, 8, 128], BF16)
            gv = sbb("gv", [128, D], F32)
            wsTf = gv.re("p (g i) -> p g i", g=8)
            kb.dma("sp", wsTf, T(wsT_d.rearrange("g j i -> j g i"), None), "wsTf")
            for g in range(8):
                kb.tt("dve", wsT[:, g, :], wsTf[:, g, :], sgumask, ALU.mult)
            onesb = sbb("onesb", [128, 2], BF16)
            kb.memset("dve", onesb, 1.0)
            l2 = sbb("l2", [2, 8, 128], F32)
            r2 = sbb("r2", [2, 8, 128], F32)
            kb.memset("dve", l2, 1.0)
            kb.dma("sp", l2[0:1], T(sgb_row_d.rearrange("(o g) c -> o g c", o=1), None), "l2")
            kb.dma("sp", r2[1:2], T(bs_d.rearrange("(o g) c -> o g c", o=1), None), "r2")
            Bias = sbb("Bias", [128, 8, 128], F32)
            for g in range(8):
                pb = ps()
                kb.mm(pb[0:1, 0:128], onesb[:, 0:1], wsT[:, g, :])
                kb.copy("dve", r2[0:1, g, :], pb[0:1, 0:128])
            for g in range(8):
                pb = ps()
                kb.mm(pb[:, 0:128], l2[0:2, g, :], r2[0:2, g, :])
                kb.copy("dve", Bias[:, g, :], pb[:, 0:128])
            Gt = [sbb(f"Gt{i}", [128, 128], F32) for i in range(2)]
            PTbd = sbb("PTbd", [128, 128], F32)
            kb.memset("dve", PTbd, 0.0)
            identf2 = sbb("identf2", [128, 128], F32)
            kb.copy("dve", identf2, ident)
            Pbd = sbb("Pbd", [128, 128], F32)
            Scur = sbb("Scur", [128, 64], F32)
            SinT = sbb("SinT", [128, 8, 64], F32)
            kb.memset("dve", SinT, 0.0)
            Sinbd = sbb("Sinbd", [128, 8, 128], BF16)
            kb.memset("dve", Sinbd, 0.0)
            gi = 0
            for bb in range(2):
                for p in range(8):
                    for k in range(1, 4):
                        c = 4 * bb + k - 1
                        gt = Gt[gi % 2]
                        gi += 1
                        r0 = c * 1024 + p * 128
                        kb.dma("sp", gt, T(cc_out[r0:r0 + 128, :], ("cc_out",)), ("gt", gi % 2))
                        if k == 1:
                            kb.copy("dve", Scur, gt[:, 0:64])
                        else:
                            kb.copy("dve", PTbd[0:64, 0:64], gt[0:64, 64:128])
                            kb.copy("dve", PTbd[64:128, 64:128], gt[64:128, 64:128])
                            pb = ps()
                            kb.mm(pb[:, 0:128], PTbd, identf2)
                            kb.copy("dve", Pbd, pb[:, 0:128])
                            pb = ps()
                            kb.mm(pb[:, 0:64], Pbd, Scur)
                            kb.tt("dve", Scur, pb[:, 0:64], gt[:, 0:64], ALU.add)
                        idx = bb * 3 + k - 1
                        kb.stt("dve", SinT[:, p, :], Scur, cmask[:, 1 + idx:2 + idx], SinT[:, p, :], ALU.mult, ALU.add)
            for p in range(8):
                kb.copy("dve", Sinbd[0:64, p, 0:64], SinT[0:64, p, :])
                kb.copy("dve", Sinbd[64:128, p, 64:128], SinT[64:128, p, :])

            wBbs = [sbb(f"wBb{i}", [128, 8, D], BF16) for i in range(2)]
            xt = sbb("xt", [128, 4, D], F32)
            xnb2 = sbb("xnb2", [128, D], BF16)
            xT = sbb("xT", [128, 8, NB], BF16)
            ufm = sbb("ufm", [128, 8, NB], BF16)
            gaf = sbb("gaf", [128, 8, NB], BF16)
            gbf = sbb("gbf", [128, 8, NB], BF16)
            vn = sbb("vn", [128, D], BF16)
            actb = sbb("actb", [128, NFC, NB], BF16)
            Gb = [sbb(f"Gb{i}", [128, NB + 2], F32) for i in range(2)]
            ct = [sbb(f"ct{i}", [128, NB], F32) for i in range(1)]
            Ghalo = sbb("Ghalo", [128, NFC, 2], F32)
            kb.memset("dve", Ghalo, 0.0)
            NWU, NWD = 3, 4
            wub = [[sbb(f"wub{i}{j}", [128, 8, 128], BF16) for j in range(2)] for i in range(NWU)]
            wdb = [sbb(f"wdb{i}", [128, D], BF16) for i in range(NWD)]
            tiles_B = [(1, 1, True)] + [(2 + 4 * tb, 4, False) for tb in range(nbt)]
            PIECE_COL = [0, 1024, 2048, 3072, -1]
            st_ = {"wB": 0, "wu": 0, "wd": 0}
            n_wB = 5 * len(tiles_B)
            n_wu = NFC * len(tiles_B)
            n_wd = NFC * (len(tiles_B) - 1)
            wudv = wud.rearrange("(dc q) c -> q dc c", q=128)

            def pre_wB(k):
                while st_["wB"] <= min(k, n_wB - 1):
                    i = st_["wB"]
                    buf = wBbs[i % 2]
                    c0_ = PIECE_COL[i % 5]
                    for dc in range(8):
                        src_ = wod[dc * 128:(dc + 1) * 128, :] if c0_ < 0 else wBd[dc * 128:(dc + 1) * 128, c0_:c0_ + 1024]
                        kb.dma(qfor(("wBb", i % 2, dc)), T(buf.ap[:, dc, :], (buf.key, dc)), T(src_, WK), ("wBb", i % 2, dc))
                    st_["wB"] += 1

            def pre_wu(k):
                while st_["wu"] <= min(k, n_wu - 1):
                    i = st_["wu"]
                    fc = i % NFC
                    wg, wv = wub[i % NWU]
                    kb.dma(qfor(("wub", i % NWU, 0)), wg, T(wudv[:, :, fc * 128:(fc + 1) * 128], WK), ("wub", i % NWU, 0))
                    kb.dma(qfor(("wub", i % NWU, 1)), wv, T(wudv[:, :, DFF + fc * 128:DFF + (fc + 1) * 128], WK), ("wub", i % NWU, 1))
                    st_["wu"] += 1

            def pre_wd(k):
                while st_["wd"] <= min(k, n_wd - 1):
                    i = st_["wd"]
                    fc = i % NFC
                    kb.dma(qfor(("wdb", i % NWD)), wdb[i % NWD], T(wdd[fc * 128:(fc + 1) * 128, :], WK), ("wdb", i % NWD))
                    st_["wd"] += 1
            cnt_ = {"wB": 0, "wu": 0, "wd": 0}
            yt = [sbb(f"yt{i}", [128, NB], F32) for i in range(3)]
            yob = sbb("yob", [128, 8, NB], BF16)
            rhb = sbb("rhb", [128, NB], BF16)
            sqb = sbb("sqb", [128, NB], BF16)
            bob = sbb("bob", [128, NB], BF16)
            gob = sbb("gob", [128, NB], BF16)
            st2 = sbb("st2", [128, 2, 6], F32)
            mv2 = sbb("mv2", [128, 2], F32)
            rs2 = sbb("rs2", [128, 1], F32)
            nm2 = sbb("nm2", [128, 1], F32)
            ztmp = sbb("ztmp", [128, 128], F32)

            def ln_rows(src, dst_f32, gB, bB, dst_bf=None):
                for hh in range(2):
                    o, i = st2.ap[:, hh, :], src.ap[:, hh * 512:(hh + 1) * 512]
                    S.op("dve", lambda e, o=o, i=i: e.bn_stats(o, i), [src.key], [st2.key])
                o, i = mv2.ap, st2.ap
                S.op("dve", lambda e, o=o, i=i: e.bn_aggr(o, i), [st2.key], [mv2.key])
                kb.act(rs2, mv2[:, 1:2], AF.Ln, bias=epsc[:, 0:1])
                kb.act(rs2, rs2, AF.Exp, scale=-0.5)
                kb.ts("dve", dst_f32, src, mv2[:, 0:1], rs2, ALU.subtract, ALU.mult)
                if dst_bf is not None:
                    kb.copy("act", dst_bf, dst_f32)
                kb.tt("dve", dst_f32, dst_f32, gB, ALU.mult)
                kb.tt("pool", dst_f32, dst_f32, bB, ALU.add)

            def b3a(c0, nsub):
                N = nsub * 128
                col = (c0 - 1) * 128
                for p in range(8):
                    cs = slice(p * 128, (p + 1) * 128)
                    y_, yc, sq = [t_[:, 0:N] for t_ in yt]
                    tq = sq
                    kb.dma("sp", y_, T(ydram[cs, col:col + N], None), "ld_y")
                    kb.dma("act", rhb[:, 0:N], T(rdram[cs, col:col + N], None), "ld_r")
                    kb.dma("pool", bob[:, 0:N], T(bdram[cs, col:col + N], None), "ld_b")
                    kb.dma("pool", gob[:, 0:N], T(gdram[cs, col:col + N], None), "ld_g")
                    pb = ps()
                    kb.mm(pb[:, 0:N], Sinbd[:, p, :], rhb[:, 0:N])
                    kb.tt("dve", y_, pb[:, 0:N], y_, ALU.add)
                    pb = ps()
                    kb.mm(pb[:, 0:N], blkf, y_)
                    kb.stt("dve", yc, pb[:, 0:N], -1.0 / 64, y_, ALU.mult, ALU.add)
                    kb.act(sqb[:, 0:N], yc, AF.Square)
                    pb = ps()
                    kb.mm(pb[:, 0:N], blkb, sqb[:, 0:N])
                    kb.act(tq, pb[:, 0:N], AF.Ln, bias=epsc[:, 2:3], scale=1.0 / 64)
                    kb.act(tq, tq, AF.Exp, scale=-0.5)
                    kb.tt("dve", yc, yc, tq, ALU.mult)
                    kb.act(yc, yc, AF.Identity, bias=pv("lxb", p), scale=pv("lxg", p))
                    kb.tt("pool", yc, yc, bob[:, 0:N], ALU.add)
                    kb.tt("pool", T(yob.ap[:, p, 0:N], ("yob", p)), yc, gob[:, 0:N], ALU.mult)
                    yield

            def phaseB_tile(c0, nsub, is_halo):
                N = nsub * 128
                for sb_ in range(nsub):
                    cc = c0 + sb_
                    xs_t = T(xt.ap[:, sb_, :], ("xt", sb_))
                    kb.dma("sp", xs_t, T(xs[cc * 128:(cc + 1) * 128, :], None), ("xt", sb_))
                    ln_rows(xs_t, xs_t, bcs[0], bcs[1], dst_bf=xnb2)
                    for g in range(2):
                        pb = ps()
                        for k in range(4):
                            dc = g * 4 + k
                            kb.mm(pb[:, k * 128:(k + 1) * 128], xnb2[:, dc * 128:(dc + 1) * 128], ident)
                        for k in range(4):
                            dc = g * 4 + k
                            kb.act(T(xT.ap[:, dc, sb_ * 128:(sb_ + 1) * 128], ("xT", sb_)),
                                   pb[:, k * 128:(k + 1) * 128], AF.Identity, bias=pv("lib", dc), scale=pv("lig", dc))
                xTk = [("xT", i) for i in range(nsub)]

                cur = {}

                def load_wB(col0):
                    k = cnt_["wB"]
                    cnt_["wB"] += 1
                    pre_wB(k + 1)
                    cur["wBb"] = wBbs[k % 2]

                def proj_fm(dst, func, bias_name):
                    wBb = cur["wBb"]
                    for g in range(8):
                        pb = ps()
                        for dc in range(8):
                            o, l, r = pb.ap[:, 0:N], wBb.ap[:, dc, g * 128:(g + 1) * 128], xT.ap[:, dc, 0:N]
                            S.op("pe", lambda e, o=o, l=l, r=r, st=(dc == 0), sp=(dc == 7):
                                 e.matmul(o, l, r, start=st, stop=sp), [(wBb.key, dc)] + xTk, [pb.key])
                        if bias_name is None:
                            kb.act(T(dst.ap[:, g, 0:N], (dst.key, g)), pb[:, 0:N], func)
                        else:
                            kb.act(T(dst.ap[:, g, 0:N], (dst.key, g)), pb[:, 0:N], func, bias=bias_name(g))
                load_wB(0)
                proj_fm(ufm, AF.Gelu, None)
                load_wB(1024)
                for sb_ in range(nsub):
                    for hh in range(2):
                        pb = ps()
                        wBb = cur["wBb"]
                        for dc in range(8):
                            o, l, r = pb.ap, xT.ap[:, dc, sb_ * 128:(sb_ + 1) * 128], wBb.ap[:, dc, hh * 512:(hh + 1) * 512]
                            S.op("pe", lambda e, o=o, l=l, r=r, st=(dc == 0), sp=(dc == 7):
                                 e.matmul(o, l, r, start=st, stop=sp), [(wBb.key, dc), ("xT", sb_)], [pb.key])
                        kb.act(gv[:, hh * 512:(hh + 1) * 512], pb, AF.Gelu)
                    for hh in range(2):
                        o, i = st2.ap[:, hh, :], gv.ap[:, hh * 512:(hh + 1) * 512]
                        S.op("dve", lambda e, o=o, i=i: e.bn_stats(o, i), [gv.key], [st2.key])
                    o, i = mv2.ap, st2.ap
                    S.op("dve", lambda e, o=o, i=i: e.bn_aggr(o, i), [st2.key], [mv2.key])
                    kb.act(rs2, mv2[:, 1:2], AF.Ln, bias=epsc[:, 0:1])
                    kb.act(rs2, rs2, AF.Exp, scale=-0.5)
                    kb.ts("dve", vn, gv, mv2[:, 0:1], rs2, ALU.subtract, ALU.mult)
                    for g in range(8):
                        pb = ps()
                        kb.mm(pb[:, 0:128], vn[:, g * 128:(g + 1) * 128], wsT[:, g, :])
                        kb.stt("dve", ztmp, pb[:, 0:128], pv("sg", g), Bias[:, g, :], ALU.mult, ALU.add)
                        uu = T(ufm.ap[:, g, sb_ * 128:(sb_ + 1) * 128], ("ufm", g))
                        kb.tt("pool", uu, ztmp, uu, ALU.mult)
                load_wB(2048)
                proj_fm(gaf, AF.Sigmoid, lambda g: pv("bg", g))
                load_wB(3072)
                proj_fm(gbf, AF.Sigmoid, lambda g: pv("bg", 8 + g))
                for g in range(8):
                    a_ = T(gaf.ap[:, g, 0:N], ("gaf", g))
                    kb.tt("pool", a_, a_, T(ufm.ap[:, g, 0:N], ("ufm", g)), ALU.mult)
                for p in range(8):
                    b_ = T(gbf.ap[:, p, 0:N], ("gbf", p))
                    kb.tt("pool", b_, T(yob.ap[:, p, 0:N], ("yob", p)), b_, ALU.mult)
                    kb.tt("dve", b_, b_, T(gaf.ap[:, p, 0:N], ("gaf", p)), ALU.add)
                load_wB(-1)
                wo = cur["wBb"]
                wops = {}
                for sb_ in range(nsub):
                    for hh in range(2):
                        pb = ps()
                        for c in range(8):
                            o, l, r = pb.ap, gbf.ap[:, c, sb_ * 128:(sb_ + 1) * 128], wo.ap[:, c, hh * 512:(hh + 1) * 512]
                            S.op("pe", lambda e, o=o, l=l, r=r, st=(c == 0), sp=(c == 7):
                                 e.matmul(o, l, r, start=st, stop=sp), [(wo.key, c), ("gbf", c)], [pb.key])
                        wops[(sb_, hh)] = pb
                for sb_ in range(nsub):
                    xs_t = T(xt.ap[:, sb_, :], ("xt", sb_))
                    for hh in range(2):
                        xh = xs_t[:, hh * 512:(hh + 1) * 512]
                        kb.stt("dve", xh, xh, ALPHA, wops[(sb_, hh)], ALU.mult, ALU.add)
                    ln_rows(xs_t, xs_t, bcs[2], bcs[3], dst_bf=None)
                    kb.copy("act", xnb2, xs_t)
                    for g in range(2):
                        pb = ps()
                        for k in range(4):
                            dc = g * 4 + k
                            kb.mm(pb[:, k * 128:(k + 1) * 128], xnb2[:, dc * 128:(dc + 1) * 128], ident)
                        kb.copy("act" if g == 0 else "dve",
                                T(xT.ap[:, g * 4:(g + 1) * 4, sb_ * 128:(sb_ + 1) * 128], ("xT", sb_)),
                                pb.re("q (k t) -> q k t", k=4))
                for fc in range(NFC):
                    k = cnt_["wu"]
                    cnt_["wu"] += 1
                    pre_wu(k + NWU - 1)
                    wg, wv = wub[k % NWU]
                    pg, pvl = ps(), ps()
                    for dst_, w_ in ((pg, wg), (pvl, wv)):
                        for dc in range(8):
                            o, l, r = dst_.ap[:, 0:N], w_.ap[:, dc, :], xT.ap[:, dc, 0:N]
                            S.op("pe", lambda e, o=o, l=l, r=r, st=(dc == 0), sp=(dc == 7):
                                 e.matmul(o, l, r, start=st, stop=sp), [w_.key] + xTk, [dst_.key])
                    G = Gb[fc % 2]
                    c_ = ct[0][:, 0:N]
                    kb.copy("pool", G[:, 0:2], Ghalo[:, fc, :])
                    kb.copy("act", G[:, 2:N + 2], pg[:, 0:N])
                    kb.act(c_, pg[:, 0:N], AF.Identity, bias=pv("cb", fc), scale=pv("cw2", fc))
                    kb.stt("dve", c_, G[:, 1:N + 1], pv("cw1", fc), c_, ALU.mult, ALU.add)
                    kb.stt("dve", c_, G[:, 0:N], pv("cw0", fc), c_, ALU.mult, ALU.add)
                    kb.copy("pool", Ghalo[:, fc, :], G[:, N:N + 2])
                    if is_halo:
                        kb.ts("pool", Ghalo[:, fc, :], Ghalo[:, fc, :], cmask[:, 0:1], None, ALU.mult)
                        yield "b5"
                        continue
                    kb.act(c_, c_, AF.Gelu)
                    kb.tt("dve", T(actb.ap[:, fc, 0:N], ("actb", fc)), pvl[:, 0:N], c_, ALU.mult)
                    yield "b5"
                if is_halo:
                    return
                pds = [[ps() for hh in range(2)] for sb_ in range(nsub)]
                for fc in range(NFC):
                    k = cnt_["wd"]
                    cnt_["wd"] += 1
                    pre_wd(k + NWD - 1)
                    wd_ = wdb[k % NWD]
                    for sb_ in range(nsub):
                        for hh in range(2):
                            kb.mm(pds[sb_][hh], T(actb.ap[:, fc, sb_ * 128:(sb_ + 1) * 128], ("actb", fc)),
                                  wd_[:, hh * 512:(hh + 1) * 512], start=(fc == 0), stop=(fc == NFC - 1))
                for sb_ in range(nsub):
                    xs_t = T(xt.ap[:, sb_, :], ("xt", sb_))
                    for hh in range(2):
                        xh = xs_t[:, hh * 512:(hh + 1) * 512]
                        kb.stt("dve", xh, xh, ALPHA, pds[sb_][hh], ALU.mult, ALU.add)
                    ln_rows(xs_t, xs_t, bcs[4], bcs[5], dst_bf=None)
                    row = (c0 - 2 + sb_) * 128
                    kb.dma("sp", T(out_d[row:row + 128, :], ("out", row)), xs_t, ("st_out", sb_))

            g3 = b3a(tiles_B[0][0], tiles_B[0][1])
            for _ in g3:
                pass
            for ti, (c0_, ns_, ih_) in enumerate(tiles_B):
                g3 = b3a(tiles_B[ti + 1][0], tiles_B[ti + 1][1]) if ti + 1 < len(tiles_B) else None
                nb5 = 0
                for ev in phaseB_tile(c0_, ns_, ih_):
                    if ev == "b5" and g3 is not None:
                        nb5 += 1
                        if nb5 % 2 == 0:
                            next(g3, None)
                if g3 is not None:
                    for _ in g3:
                        pass

        S.barrier()
        S.q["sp"].append((S.pending_barrier["sp"], None, None))

        S.alloc_sems()
        with nc.Block() as block:
            @block.tensor
            def _(e):
                S.replay("pe", e)

            @block.scalar
            def _(e):
                S.replay("act", e)

            @block.vector
            def _(e):
                S.replay("dve", e)

            @block.gpsimd
            def _(e):
                S.replay("pool", e)

            @block.sync
            def _(e):
                S.replay("sp", e)
    return nc


def _consts():
    s = np.arange(128)
    strict = (s[:, None] < s[None, :]).astype(np.float32)
    incl = (s[:, None] <= s[None, :]).astype(np.float32)
    cf = np.zeros((128, 1280), np.float32)
    cf[:, 0:128] = strict
    cf[:, 128:256] = incl
    cf[:, 256:384] = strict
    cf[:, 384:512] = incl
    for q in range(4):
        cf[:, 512 + q * 128:512 + (q + 1) * 128] = strict.T
    e = np.float32(-np.exp(-0.5))
    cf[:, 1024:1152] = incl * e
    cf[:, 1152:1280] = strict * e
    cb = np.zeros((128, 1024), np.float32)
    cb[:, 768:896] = incl
    cb[:, 896:1024] = strict
    for q in range(4):
        cb[:, q * 128:(q + 1) * 128] = np.eye(128)
    blk = np.zeros((128, 128), np.float32)
    blk[0:64, 0:64] = 1.0
    blk[64:128, 64:128] = 1.0
    cb[:, 512:640] = blk
    ch = s // 64
    cb[:, 640:768] = (ch[None, :] >= ch[:, None]).astype(np.float32)
    return cf, cb.astype(ml_dtypes.bfloat16)


def _chunkcols(v, n):
    v = np.asarray(v, np.float32).reshape(-1)
    pad = np.zeros(n * 128, np.float32)
    pad[:v.size] = v
    return pad.reshape(n, 128).T


def prep_inputs(inp):
    f = lambda a: np.ascontiguousarray(np.asarray(a, np.float32))
    x = f(inp["x"])
    cf, cb = _consts()
    pvec = np.zeros((128, NPV), np.float32)

    def put(name, v, n):
        pvec[:, PV[name]:PV[name] + n] = _chunkcols(v, n)
    put("mu", inp["mu_shift"][0], 27)
    put("bg", inp["b_gate"][0], 16)
    put("sg", inp["sgu_ln_g"][0], 8)
    put("sb", inp["sgu_ln_b"][0], 8)
    put("a0", inp["a0"][0], 8)
    put("kk", inp["k_k"][0], 8)
    put("ka", inp["k_a"][0], 8)
    put("rk", f(inp["r_k"][0]).reshape(-1), 8)
    put("lxg", inp["lnx_g"][0], 8)
    put("lxb", inp["lnx_b"][0], 8)
    put("lig", inp["ln_in_g"], 8)
    put("lib", inp["ln_in_b"], 8)
    cw = f(inp["conv_w"][0])
    put("cw0", cw[0], 21)
    put("cw1", cw[1], 21)
    put("cw2", cw[2], 21)
    put("cb", inp["conv_b"][0], 21)
    w2aug = np.concatenate([f(inp["w2"][0]), f(inp["w0"][0]).reshape(1, D)], 0)
    bc = np.stack([np.broadcast_to(f(v).reshape(1, D), (128, D)) for v in
                   (inp["ln_in_g"], inp["ln_in_b"], inp["ln1_g"][0], inp["ln1_b"][0],
                    inp["ln2_g"][0], inp["ln2_b"][0])], 0)
    shared = {
        "pvec": pvec, "w_in": f(inp["w_in"][0]), "w_o": f(inp["w_o"][0]), "w_up": f(inp["w_up"][0]),
        "w_down": f(inp["w_down"][0]), "w2aug": f(w2aug), "a2": f(inp["a2"][0]), "g2": f(inp["g2"][0]),
        "w_sT": f(np.transpose(f(inp["w_s"][0]), (0, 2, 1))), "b_s": f(inp["b_s"][0]),
        "sgb_row": f(inp["sgu_ln_b"][0]).reshape(8, 128), "bcast": f(bc), "cf32": cf, "cbf16": cb,
    }
    maps = []
    for c in range(NCORES):
        b, q = c // 4, c % 4
        t0 = q * OWN
        xs = np.zeros((ROWS, D), np.float32)
        lo = t0 - 256
        if lo >= 0:
            xs[:] = x[b, lo:t0 + OWN]
        else:
            xs[256:] = x[b, 0:OWN]
        cm = np.zeros((128, 16), np.float32)
        cm[:, 0] = 0.0 if q == 0 else 1.0
        if q > 0:
            cm[:, 1 + (b * 3 + q - 1)] = 1.0
        m = dict(shared)
        m["xs"] = xs
        m["cmask"] = cm
        maps.append(m)
    return maps


_NC_CACHE = {}


def kernel(**inputs):
    if "nc" not in _NC_CACHE:
        _NC_CACHE["nc"] = build_program(False)
    nc = _NC_CACHE["nc"]
    maps = prep_inputs(inputs)
    res = run_bass_kernel_spmd(nc, maps, core_ids=list(range(NCORES)))
    out = np.zeros((2, SEQ, D), np.float32)
    for c in range(NCORES):
        b, q = c // 4, c % 4
        out[b, q * OWN:(q + 1) * OWN] = res.results[c]["out"]
    return out
```
